# Optimizing a Trainium2 kernel written in Bass

```python
import jax, jax.numpy as jnp
from jax import lax
import numpy as np

D_MODEL = 1024
BATCH = 4
SEQ = 8192
DEPTH = 4

HG_HEADS = 8
HG_DK = 128
HG_DV = D_MODEL // HG_HEADS
HG_CHUNK = 64
LB_FLOOR = 1e-30
MLA_HEADS = 8
MLA_NOPE = 128
MLA_ROPE = 64
MLA_V = 128
Q_LORA = 384
KV_LORA = 256
ROPE_THETA = 10000.0
Q_BLOCK = 128
MASK_VALUE = -1e30
D_FF = 2816
DN_ALPHA = (2.0 * DEPTH) ** 0.25
DN_BETA = (8.0 * DEPTH) ** -0.25
LN_EPS = 1e-5
RMS_EPS = 1e-6
N_ADA = 9

SPLITS = (HG_HEADS * HG_DK,
          HG_HEADS * HG_DK,
          HG_HEADS * HG_DV,
          HG_HEADS * HG_DV,
          Q_LORA,
          KV_LORA,
          MLA_ROPE,
          D_MODEL,
          D_MODEL)
D_IN = sum(SPLITS)
SPLIT_POINTS = [int(v) for v in np.cumsum(SPLITS)[:-1]]

kernel_name = "hybrid_hgrn2_mla_macaron_deepnorm_adaln"


def layer_norm(x, g, b):
    xf = x.astype(jnp.float32)
    mu = jnp.mean(xf, -1, keepdims=True)
    var = jnp.mean(jnp.square(xf - mu), -1, keepdims=True)
    return ((xf - mu) * lax.rsqrt(var + LN_EPS)).astype(x.dtype) * g + b


def rms_norm(x, g):
    xf = x.astype(jnp.float32)
    return (xf * lax.rsqrt(jnp.mean(xf * xf, -1, keepdims=True) + RMS_EPS)).astype(x.dtype) * g


def rope_angles(positions):
    inv = 1.0 / (ROPE_THETA ** (jnp.arange(0, MLA_ROPE, 2, dtype=jnp.float32) / MLA_ROPE))
    ang = positions.astype(jnp.float32)[..., None] * inv
    return jnp.cos(ang), jnp.sin(ang)


def apply_rope(x, cos, sin):
    x1, x2 = jnp.split(x, 2, axis=-1)
    cos = cos.astype(x.dtype)
    sin = sin.astype(x.dtype)
    return jnp.concatenate([x1 * cos - x2 * sin, x1 * sin + x2 * cos], axis=-1)


def swiglu(u, w_gate, w_up, w_down):
    return (jax.nn.silu(u @ w_gate) * (u @ w_up)) @ w_down


def hgrn2_recurrence(q, k, v, log_f):
    B, S, H, DK = q.shape
    DV = v.shape[-1]
    n = S // HG_CHUNK

    def chunks(t):
        return t.astype(jnp.float32).reshape(B, n, HG_CHUNK, H, t.shape[-1]).transpose(1, 0, 3, 2, 4)

    qc, kc, vc = chunks(q), chunks(k), chunks(v)
    G = jnp.cumsum(chunks(log_f), axis=3)
    causal = jnp.tril(jnp.ones((HG_CHUNK, HG_CHUNK), bool))[:, :, None]

    def step(state, inp):
        qb, kb, vb, Gb = inp
        diff = Gb[:, :, :, None, :] - Gb[:, :, None, :, :]
        decay = jnp.where(causal, jnp.exp(jnp.where(causal, diff, 0.0)), 0.0)
        scores = jnp.einsum('bhtd,bhsd,bhtsd->bhts', qb, kb, decay)
        o = (jnp.einsum('bhts,bhse->bhte', scores, vb)
             + jnp.einsum('bhtd,bhde->bhte', qb * jnp.exp(Gb), state))
        G_end = Gb[:, :, -1:, :]
        state = (jnp.exp(G_end[:, :, 0, :])[..., None] * state
                 + jnp.einsum('bhsd,bhse->bhde', kb * jnp.exp(G_end - Gb), vb))
        return state, o

    state0 = jnp.zeros((B, H, DK, DV), jnp.float32)
    _, o = lax.scan(step, state0, (qc, kc, vc, G))
    return o.transpose(1, 0, 3, 2, 4).reshape(B, S, H, DV).astype(v.dtype)


def causal_block_attention(q, k, v):
    S = q.shape[1]
    scale = (MLA_NOPE + MLA_ROPE) ** -0.5
    outs = []
    for blk in range(S // Q_BLOCK):
        q0 = blk * Q_BLOCK
        kend = q0 + Q_BLOCK
        s = jnp.einsum('bqhd,bkhd->bhqk', q[:, q0:kend], k[:, :kend]).astype(jnp.float32) * scale
        mask = (q0 + jnp.arange(Q_BLOCK))[:, None] >= jnp.arange(kend)[None, :]
        p = jax.nn.softmax(jnp.where(mask, s, MASK_VALUE), axis=-1).astype(v.dtype)
        outs.append(jnp.einsum('bhqk,bkhd->bqhd', p, v[:, :kend]))
    return jnp.concatenate(outs, axis=1)


def token_mixer(u, cos, sin, lb, w_in, hg_norm_g, q_norm_g, w_uq, kv_norm_g, w_ukv,
                w_branch_hg, w_branch_mla, w_out):
    B, S, _ = u.shape
    zq, zf, zi, zg, cq, ckv, kr, ga, gb = jnp.split(u @ w_in, SPLIT_POINTS, axis=-1)

    lbh = lb.reshape(HG_HEADS, HG_DK)
    zf32 = zf.astype(jnp.float32).reshape(B, S, HG_HEADS, HG_DK)
    log_f = jnp.logaddexp(jnp.log(jnp.maximum(lbh, LB_FLOOR)),
                          jnp.log1p(-lbh) + jax.nn.log_sigmoid(zf32))
    hk = ((1.0 - lbh) * jax.nn.sigmoid(-zf32)).astype(u.dtype)
    hq = jax.nn.silu(zq).reshape(B, S, HG_HEADS, HG_DK)
    hv = zi.reshape(B, S, HG_HEADS, HG_DV)
    o_hg = hgrn2_recurrence(hq, hk, hv, log_f)
    o_hg = rms_norm(o_hg, hg_norm_g) * jax.nn.silu(zg).reshape(B, S, HG_HEADS, HG_DV)
    y_hg = o_hg.reshape(B, S, HG_HEADS * HG_DV) @ w_branch_hg

    q = (rms_norm(cq, q_norm_g) @ w_uq).reshape(B, S, MLA_HEADS, MLA_NOPE + MLA_ROPE)
    q = jnp.concatenate([q[..., :MLA_NOPE],
                         apply_rope(q[..., MLA_NOPE:], cos[:, :, None], sin[:, :, None])], axis=-1)
    kv = (rms_norm(ckv, kv_norm_g) @ w_ukv).reshape(B, S, MLA_HEADS, MLA_NOPE + MLA_V)
    k_rope = apply_rope(kr, cos, sin)[:, :, None, :]
    k = jnp.concatenate([kv[..., :MLA_NOPE],
                         jnp.broadcast_to(k_rope, (B, S, MLA_HEADS, MLA_ROPE))], axis=-1)
    v = kv[..., MLA_NOPE:]
    o_mla = causal_block_attention(q, k, v)
    y_mla = o_mla.reshape(B, S, MLA_HEADS * MLA_V) @ w_branch_mla

    merged = jax.nn.sigmoid(ga) * y_hg + jax.nn.sigmoid(gb) * y_mla
    return merged @ w_out


def setup_inputs(seed: int = 0) -> dict:
    key = jax.random.key(seed)
    ks = iter(jax.random.split(key, 32))

    def nrm(shape, fan_in, gain=1.0):
        return jax.random.normal(next(ks), shape, jnp.float32) * (gain * fan_in ** -0.5)

    def gains(shape):
        return 1.0 + 0.02 * jax.random.normal(next(ks), shape, jnp.float32)

    def small(shape, s=0.02):
        return s * jax.random.normal(next(ks), shape, jnp.float32)

    x = jax.random.normal(next(ks), (BATCH, SEQ, D_MODEL), jnp.float32)
    c = jax.random.normal(next(ks), (BATCH, D_MODEL), jnp.float32)
    offset = jax.random.randint(next(ks), (BATCH, 1), 0, 4096)
    positions = (jnp.arange(SEQ, dtype=jnp.int32)[None, :] + offset).astype(jnp.int32)
    return {
        "x": x,
        "c": c,
        "positions": positions,
        "ada_w": nrm((DEPTH, D_MODEL, N_ADA * D_MODEL), D_MODEL, 0.2),
        "ada_b": small((DEPTH, N_ADA * D_MODEL)),
        "ln_g": gains((DEPTH, 3, D_MODEL)),
        "ln_b": small((DEPTH, 3, D_MODEL)),
        "ffn1_gate": nrm((DEPTH, D_MODEL, D_FF), D_MODEL),
        "ffn1_up": nrm((DEPTH, D_MODEL, D_FF), D_MODEL),
        "ffn1_down": nrm((DEPTH, D_FF, D_MODEL), D_FF, DN_BETA),
        "w_in": nrm((DEPTH, D_MODEL, D_IN), D_MODEL),
        "hg_lower_bound": small((DEPTH, HG_HEADS * HG_DK), 0.1),
        "hg_norm_g": gains((DEPTH, HG_HEADS, HG_DV)),
        "mla_q_norm_g": gains((DEPTH, Q_LORA)),
        "mla_w_uq": nrm((DEPTH, Q_LORA, MLA_HEADS * (MLA_NOPE + MLA_ROPE)), Q_LORA),
        "mla_kv_norm_g": gains((DEPTH, KV_LORA)),
        "mla_w_ukv": nrm((DEPTH, KV_LORA, MLA_HEADS * (MLA_NOPE + MLA_V)), KV_LORA),
        "w_branch_hg": nrm((DEPTH, HG_HEADS * HG_DV, D_MODEL), HG_HEADS * HG_DV),
        "w_branch_mla": nrm((DEPTH, MLA_HEADS * MLA_V, D_MODEL), MLA_HEADS * MLA_V),
        "w_out": nrm((DEPTH, D_MODEL, D_MODEL), D_MODEL, DN_BETA),
        "ffn2_gate": nrm((DEPTH, D_MODEL, D_FF), D_MODEL),
        "ffn2_up": nrm((DEPTH, D_MODEL, D_FF), D_MODEL),
        "ffn2_down": nrm((DEPTH, D_FF, D_MODEL), D_FF, DN_BETA),
    }


def reference(x, c, positions, ada_w, ada_b, ln_g, ln_b, ffn1_gate, ffn1_up, ffn1_down, w_in,
              hg_lower_bound, hg_norm_g, mla_q_norm_g, mla_w_uq, mla_kv_norm_g, mla_w_ukv,
              w_branch_hg, w_branch_mla, w_out, ffn2_gate, ffn2_up, ffn2_down):
    B = x.shape[0]
    cos, sin = rope_angles(positions)
    lb_soft = jax.nn.softmax(hg_lower_bound.astype(jnp.float32), axis=0)
    lower = jnp.cumsum(lb_soft, axis=0) - lb_soft[0]
    cond = jax.nn.silu(c)
    h = x
    for l in range(DEPTH):
        ada = (cond @ ada_w[l] + ada_b[l]).reshape(B, 1, N_ADA, D_MODEL)

        u = h * (1.0 + ada[:, :, 1]) + ada[:, :, 0]
        y = swiglu(u, ffn1_gate[l], ffn1_up[l], ffn1_down[l])
        h = layer_norm(DN_ALPHA * h + 0.5 * (1.0 + ada[:, :, 2]) * y, ln_g[l, 0], ln_b[l, 0])

        u = h * (1.0 + ada[:, :, 4]) + ada[:, :, 3]
        y = token_mixer(u, cos, sin, lower[l], w_in[l], hg_norm_g[l], mla_q_norm_g[l], mla_w_uq[l],
                        mla_kv_norm_g[l], mla_w_ukv[l], w_branch_hg[l], w_branch_mla[l], w_out[l])
        h = layer_norm(DN_ALPHA * h + (1.0 + ada[:, :, 5]) * y, ln_g[l, 1], ln_b[l, 1])

        u = h * (1.0 + ada[:, :, 7]) + ada[:, :, 6]
        y = swiglu(u, ffn2_gate[l], ffn2_up[l], ffn2_down[l])
        h = layer_norm(DN_ALPHA * h + 0.5 * (1.0 + ada[:, :, 8]) * y, ln_g[l, 2], ln_b[l, 2])
    return h
```

```python
import contextlib
import math
import numpy as np
import ml_dtypes
import concourse.bass as bass
import concourse.mybir as mybir
from concourse.bass_utils import run_bass_kernel_spmd

F32 = mybir.dt.float32
BF16 = mybir.dt.bfloat16
I32 = mybir.dt.int32
AF = mybir.ActivationFunctionType
ALU = mybir.AluOpType
AX = mybir.AxisListType

D = 1024
FF = 2816
NJ = FF // 128
DIN = 6848
NH = 8
DEPTH_FULL = 4
ALPHA = (2.0 * DEPTH_FULL) ** 0.25
LN_EPS = 1e-5
RMS_EPS = 1e-6
ATT_SCALE = 192.0 ** -0.5
NT = 512
ENGS = ("pe", "act", "dve", "pool", "sp")


class Res:
    __slots__ = ("writer", "readers")

    def __init__(self):
        self.writer = None
        self.readers = []


class Buf:
    __slots__ = ("ap", "res")

    def __init__(self, ap):
        self.ap = ap
        self.res = Res()

    def __getitem__(self, k):
        return self.ap[k]


class Op:
    __slots__ = ("eng", "fn", "deps", "is_dma", "has_dep", "sem", "val", "dslot", "prev_on_sem", "phase")

    def __init__(self, eng, fn, is_dma, phase):
        self.eng = eng
        self.fn = fn
        self.deps = []
        self.is_dma = is_dma
        self.has_dep = is_dma
        self.sem = None
        self.val = None
        self.prev_on_sem = None
        self.phase = phase


class Ctx:
    def __init__(self, nc, es, n_dma_sems=32):
        self.nc = nc
        self.esem = {e: es.enter_context(nc.semaphore("s_" + e)) for e in ENGS}
        self.dsems = [es.enter_context(nc.semaphore("d%d" % i)) for i in range(n_dma_sems)]
        self.tick = {e: 0 for e in ENGS}
        self.dcount = [0] * n_dma_sems
        self.dlast = [None] * n_dma_sems
        self.dma_rr = 0
        self.sw_rr = 0
        self.n_sw = 8
        self.n_hw = n_dma_sems - self.n_sw
        self.phase = 0
        self.n_inst = 0


class Sched:
    def __init__(self, ctx):
        self.ctx = ctx
        self.ops = []
        ctx.phase += 1
        self.phase = ctx.phase

    def _add(self, op, reads, writes):
        deps = {}
        for r in reads:
            r = r.res if isinstance(r, Buf) else r
            if r.writer is not None:
                deps[id(r.writer)] = r.writer
        for r in writes:
            r = r.res if isinstance(r, Buf) else r
            if r.writer is not None:
                deps[id(r.writer)] = r.writer
            for o in r.readers:
                deps[id(o)] = o
        for d in deps.values():
            if d is op or d.phase != self.phase:
                continue
            if d.eng == op.eng and op.eng == "pe" and not d.is_dma and not op.is_dma:
                continue
            op.deps.append(d)
            d.has_dep = True
        for r in reads:
            r = r.res if isinstance(r, Buf) else r
            r.readers.append(op)
        for r in writes:
            r = r.res if isinstance(r, Buf) else r
            r.writer = op
            r.readers = []
        self.ops.append(op)
        return op

    def op(self, eng, fn, r=(), w=()):
        return self._add(Op(eng, fn, False, self.phase), r, w)

    def dma(self, eng, fn, r=(), w=()):
        op = Op(eng, fn, True, self.phase)
        ctx = self.ctx
        if eng == "pool":
            op.dslot = ctx.n_hw + ctx.sw_rr % ctx.n_sw
            ctx.sw_rr += 1
        else:
            op.dslot = ctx.dma_rr % ctx.n_hw
            ctx.dma_rr += 1
        return self._add(op, r, w)

    def emit(self):
        ctx = self.ctx
        nc = ctx.nc
        for op in self.ops:
            if op.is_dma:
                k = op.dslot
                prev = ctx.dlast[k]
                op.prev_on_sem = prev if (prev is not None and prev.phase == self.phase) else None
                ctx.dcount[k] += 16
                op.sem = ctx.dsems[k]
                op.val = ctx.dcount[k]
                ctx.dlast[k] = op
            elif op.has_dep:
                ctx.tick[op.eng] += 1
                op.sem = ctx.esem[op.eng]
                op.val = ctx.tick[op.eng]
        per_eng = {e: [] for e in ENGS}
        for op in self.ops:
            per_eng[op.eng].append(op)
        dma_ops = [op for op in self.ops if op.is_dma]
        ctx.n_inst += len(self.ops)

        def run(eng_name, eng):
            known = {}
            for op in per_eng[eng_name]:
                waits = {}
                deps = list(op.deps)
                if op.prev_on_sem is not None:
                    deps.append(op.prev_on_sem)
                for d in deps:
                    key = id(d.sem)
                    if known.get(key, 0) >= d.val:
                        continue
                    if key not in waits or waits[key][1] < d.val:
                        waits[key] = (d.sem, d.val)
                for key, (sem, val) in waits.items():
                    eng.wait_ge(sem, val)
                    known[key] = val
                ins = op.fn(eng)
                if op.is_dma:
                    ins.then_inc(op.sem, 16)
                elif op.has_dep:
                    ins.then_inc(op.sem, 1)
            if eng_name == "sp":
                last = {}
                for d in dma_ops:
                    last[id(d.sem)] = (d.sem, d.val)
                for key, (sem, val) in last.items():
                    if known.get(key, 0) < val:
                        eng.wait_ge(sem, val)

        with nc.Block() as block:
            @block.tensor
            def _(eng):
                run("pe", eng)

            @block.scalar
            def _(eng):
                run("act", eng)

            @block.vector
            def _(eng):
                run("dve", eng)

            @block.gpsimd
            def _(eng):
                run("pool", eng)

            @block.sync
            def _(eng):
                run("sp", eng)


class Phase:
    def __init__(self, G, name):
        self.G = G
        self.name = name
        self.n = 0

    def __enter__(self):
        self.es = contextlib.ExitStack()
        self.S = Sched(self.G.ctx)
        return self

    def __exit__(self, et, ev, tb):
        if et is None:
            self.S.emit()
        self.es.close()
        return False

    def sb(self, shape, dt, name=None):
        self.n += 1
        t = self.es.enter_context(self.G.nc.sbuf_tensor("%s_%s%d" % (self.name, name or "t", self.n), list(shape), dt))
        return Buf(t)

    def op(self, eng, fn, r=(), w=()):
        return self.S.op(eng, fn, r, w)

    def dma(self, eng, fn, r=(), w=()):
        return self.S.dma(eng, fn, r, w)


class Glob:
    pass


def build_program(T, depth, debug_outs=(), stop_after=None):
    nc = bass.Bass("TRN2", target_bir_lowering=False)
    G = Glob()
    G.nc, G.T, G.depth = nc, T, depth
    G.debug_outs = debug_outs

    def din(name, shape, dt=F32):
        return nc.dram_tensor(name, list(shape), dt, kind="ExternalInput").ap()

    def dscr(name, shape, dt):
        kind = "ExternalOutput" if name in debug_outs else "Internal"
        return nc.dram_tensor(name, list(shape), dt, kind=kind).ap()

    I = {}
    I["xT"] = din("xT", [D, T])
    I["ccol"] = din("ccol", [128, 8])
    I["pos"] = din("pos", [1, T], I32)
    I["ada_w"] = din("ada_w", [depth, D, 9 * D])
    I["ada_bc"] = din("ada_bc", [depth, 128, 72])
    I["lngc"] = din("lngc", [depth, 128, 24])
    I["lnbc"] = din("lnbc", [depth, 128, 24])
    for f in (1, 2):
        I["ffn%d_gate" % f] = din("ffn%d_gate" % f, [depth, D, FF])
        I["ffn%d_up" % f] = din("ffn%d_up" % f, [depth, D, FF])
        I["ffn%d_down" % f] = din("ffn%d_down" % f, [depth, FF, D])
    I["w_in"] = din("w_in", [depth, D, DIN])
    I["lbc"] = din("lbc", [depth, 128, 8])
    I["hgnc"] = din("hgnc", [depth, 128, 8])
    I["qnc"] = din("qnc", [depth, 128, 3])
    I["w_uq"] = din("w_uq", [depth, 384, 1536])
    I["kvnc"] = din("kvnc", [depth, 128, 2])
    I["w_ukv"] = din("w_ukv", [depth, 256, 2048])
    I["w_bhg"] = din("w_bhg", [depth, D, D])
    I["w_bmla"] = din("w_bmla", [depth, D, D])
    I["w_out"] = din("w_out", [depth, D, D])
    I["c_ident"] = din("c_ident", [128, 128], BF16)
    I["c_tri"] = din("c_tri", [128, 128], BF16)
    I["c_tri8"] = din("c_tri8", [64, NT], BF16)
    I["c_reset"] = din("c_reset", [128, NT])
    I["c_inv2"] = din("c_inv2", [64, 1])
    G.I = I
    G.outT = nc.dram_tensor("outT", [D, T], F32, kind="ExternalOutput").ap()

    G.WguS = [[dscr("WguS_%d_%d" % (l, f), [NJ, 128, 2, 8, 128], BF16) for f in range(2)] for l in range(depth)]
    G.WdS = [[dscr("WdS_%d_%d" % (l, f), [8, 128, NJ, 128], BF16) for f in range(2)] for l in range(depth)]
    G.WinS = [dscr("WinS_%d" % l, [47, 128, 8, 128], BF16) for l in range(depth)]
    G.WziS = [dscr("WziS_%d" % l, [D, D], BF16) for l in range(depth)]
    G.WuqS = [dscr("WuqS_%d" % l, [24, 128, 3, 128], BF16) for l in range(depth)]
    G.WuknS = [dscr("WuknS_%d" % l, [8, 128, 2, 128], BF16) for l in range(depth)]
    G.WukvV = [dscr("WukvV_%d" % l, [256, D], BF16) for l in range(depth)]
    G.WbhgS = [dscr("WbhgS_%d" % l, [D, D], BF16) for l in range(depth)]
    G.WbmlaS = [dscr("WbmlaS_%d" % l, [D, D], BF16) for l in range(depth)]
    G.WoutS = [dscr("WoutS_%d" % l, [D, D], BF16) for l in range(depth)]
    G.hA = dscr("hA", [D, T], F32)
    G.hB = dscr("hB", [D, T], F32)
    G.cosS = dscr("cosS", [64, T], F32)
    G.sinS = dscr("sinS", [64, T], F32)
    G.qtT = dscr("qtT", [NH, 128, T], BF16)
    G.ktT = dscr("ktT", [NH, 128, T], BF16)
    G.abS = dscr("abS", [T // NT, 128, NH, 3, 8], F32)
    G.vhg = dscr("vhg", [T, D], BF16)
    G.zgT = dscr("zgT", [D, T], BF16)
    G.gaT = dscr("gaT", [D, T], BF16)
    G.gbT = dscr("gbT", [D, T], BF16)
    G.qnT = dscr("qnT", [NH, 128, T], BF16)
    G.qrT = dscr("qrT", [NH, 64, T], BF16)
    G.knT = dscr("knT", [NH, 128, T], BF16)
    G.krT = dscr("krT", [64, T], BF16)
    G.vm = dscr("vm", [NH, 128, T // 128, 128], BF16)
    G.mhT = dscr("mhT", [D, T], F32)
    G.omT = dscr("omT", [NH, 128, T], BF16)

    with contextlib.ExitStack() as es:
        G.ctx = Ctx(nc, es)

        def gsb(name, shape, dt):
            return Buf(es.enter_context(nc.sbuf_tensor(name, list(shape), dt)))

        G.modc = gsb("modc", [128, depth, 9, 8], F32)
        G.omlc = gsb("omlc", [128, depth, 8], F32)
        G.lng = gsb("lng", [128, depth, 24], F32)
        G.lnb = gsb("lnb", [128, depth, 24], F32)
        G.hgn = gsb("hgn", [128, depth, 8], F32)
        G.qn = gsb("qn", [128, depth, 3], F32)
        G.kvn = gsb("kvn", [128, depth, 2], F32)
        G.ident = gsb("ident", [128, 128], BF16)
        G.tri = gsb("tri", [128, 128], BF16)
        G.tri8 = gsb("tri8", [64, NT], BF16)
        G.reset = gsb("reset", [128, NT], F32)
        G.ones_f = gsb("ones_f", [128, 128], F32)
        G.ones_b = gsb("ones_b", [128, 128], BF16)
        G.ps = [Buf(es.enter_context(nc.psum_tensor("ps%d" % i, [128, NT], F32))) for i in range(8)]

        phases = []
        phases.append(("w0", lambda: phase_w0(G)))
        phases.append(("c0", lambda: phase_c0(G)))
        for l in range(depth):
            last = l == depth - 1
            phases.append(("f1_%d" % l, lambda l=l: phase_ffn(G, l, 1, I["xT"] if l == 0 else G.hA, G.hA)))
            phases.append(("m1a_%d" % l, lambda l=l: phase_m1a(G, l, G.hA)))
            phases.append(("m1b_%d" % l, lambda l=l: phase_m1b(G, l, G.hA)))
            phases.append(("hg_%d" % l, lambda l=l: phase_hg(G, l)))
            phases.append(("a1_%d" % l, lambda l=l: phase_a1(G, l)))
            phases.append(("a2_%d" % l, lambda l=l: phase_a2(G, l, G.hA, G.hA)))
            phases.append(("f2_%d" % l, lambda l=l, last=last: phase_ffn(G, l, 2, G.hA, G.outT if last else G.hA)))
        for name, fn in phases:
            fn()
            if stop_after is not None and name == stop_after:
                break
    G.n_inst = G.ctx.n_inst
    return nc, G


WIN_CHUNKS = ([(128 * i, 128) for i in range(8)] + [(1024 + 128 * i, 128) for i in range(8)]
              + [(3072 + 128 * i, 128) for i in range(8)] + [(4096 + 128 * i, 128) for i in range(3)]
              + [(4480 + 128 * i, 128) for i in range(2)] + [(4800 + 128 * i, 128) for i in range(8)]
              + [(5824 + 128 * i, 128) for i in range(8)])
C_ZQ, C_ZF, C_ZG, C_CQ, C_CKV, C_GA, C_GB, C_KRA, C_KRB = 0, 8, 16, 24, 27, 29, 37, 45, 46


def phase_w0(G):
    I = G.I
    with Phase(G, "w0") as P:
        def cp(dst, src):
            P.dma("pool", lambda e, d=dst, s=src: e.dma_start(out=d, in_=s))

        for l in range(G.depth):
            for f in range(2):
                gate, up, down = I["ffn%d_gate" % (f + 1)][l], I["ffn%d_up" % (f + 1)][l], I["ffn%d_down" % (f + 1)][l]
                for j in range(NJ):
                    for g, W in enumerate((gate, up)):
                        cp(G.WguS[l][f][j, :, g, :, :], W[:, j * 128:(j + 1) * 128].rearrange("(k p) m -> p k m", p=128))
                for m in range(8):
                    cp(G.WdS[l][f][m], down[:, m * 128:(m + 1) * 128].rearrange("(j p) m -> p j m", p=128))
            win = I["w_in"][l]
            for c, (c0, wd) in enumerate(WIN_CHUNKS):
                cp(G.WinS[l][c], win[:, c0:c0 + wd].rearrange("(k p) m -> p k m", p=128))
            for o in (0, 64):
                cp(G.WinS[l][C_KRA, :, :, o:o + 64], win[:, 4736:4800].rearrange("(k p) m -> p k m", p=128))
                cp(G.WinS[l][C_KRB, :, :, o:o + 32], win[:, 4768:4800].rearrange("(k p) m -> p k m", p=128))
                cp(G.WinS[l][C_KRB, :, :, o + 32:o + 64], win[:, 4736:4768].rearrange("(k p) m -> p k m", p=128))
            cp(G.WziS[l], win[:, 2048:3072])
            uq = I["w_uq"][l]
            for h in range(NH):
                b = 192 * h
                cp(G.WuqS[l][3 * h], uq[:, b:b + 128].rearrange("(k p) m -> p k m", p=128))
                for o in (0, 64):
                    cp(G.WuqS[l][3 * h + 1, :, :, o:o + 64], uq[:, b + 128:b + 192].rearrange("(k p) m -> p k m", p=128))
                    cp(G.WuqS[l][3 * h + 2, :, :, o:o + 32], uq[:, b + 160:b + 192].rearrange("(k p) m -> p k m", p=128))
                    cp(G.WuqS[l][3 * h + 2, :, :, o + 32:o + 64], uq[:, b + 128:b + 160].rearrange("(k p) m -> p k m", p=128))
            ukv = I["w_ukv"][l]
            for h in range(NH):
                cp(G.WuknS[l][h], ukv[:, 256 * h:256 * h + 128].rearrange("(k p) m -> p k m", p=128))
                cp(G.WukvV[l][:, 128 * h:128 * h + 128], ukv[:, 256 * h + 128:256 * h + 256])
            cp(G.WbhgS[l], I["w_bhg"][l])
            cp(G.WbmlaS[l], I["w_bmla"][l])
            cp(G.WoutS[l], I["w_out"][l])


def phase_c0(G):
    I = G.I
    T, depth = G.T, G.depth
    with Phase(G, "c0") as P:
        ld = lambda dst, src, w: P.dma("sp", lambda e: e.dma_start(out=dst, in_=src), w=w)
        ld(G.lng[:], I["lngc"].rearrange("l p c -> p l c"), [G.lng])
        ld(G.lnb[:], I["lnbc"].rearrange("l p c -> p l c"), [G.lnb])
        ld(G.hgn[:], I["hgnc"].rearrange("l p c -> p l c"), [G.hgn])
        ld(G.qn[:], I["qnc"].rearrange("l p c -> p l c"), [G.qn])
        ld(G.kvn[:], I["kvnc"].rearrange("l p c -> p l c"), [G.kvn])
        ld(G.ident[:], I["c_ident"], [G.ident])
        ld(G.tri[:], I["c_tri"], [G.tri])
        ld(G.tri8[:], I["c_tri8"], [G.tri8])
        ld(G.reset[:], I["c_reset"], [G.reset])
        P.op("pool", lambda e: e.memset(G.ones_f[:], 1.0), w=[G.ones_f])
        P.op("pool", lambda e: e.memset(G.ones_b[:], 1.0), w=[G.ones_b])
        cc = P.sb([128, 8], F32)
        cond = P.sb([128, 8], F32)
        adab = P.sb([128, depth, 72], F32)
        ld(cc[:], I["ccol"], [cc])
        ld(adab[:], I["ada_bc"].rearrange("l p c -> p l c"), [adab])
        P.op("act", lambda e: e.activation(out=cond[:], in_=cc[:], func=AF.Silu), r=[cc], w=[cond])
        pan = [P.sb([128, 8, D], F32) for _ in range(2)]
        psA = G.ps[0]
        n = 0
        for l in range(depth):
            for j in range(9):
                pb = pan[n % 2]
                n += 1
                P.dma("sp", lambda e, pb=pb, l=l, j=j: e.dma_start(
                    out=pb[:], in_=I["ada_w"][l][:, j * D:(j + 1) * D].rearrange("(k p) n -> p k n", p=128)), w=[pb])

                def mm(e, pb=pb, l=l, j=j):
                    for m in range(8):
                        col = l * 72 + j * 8 + m
                        for k in range(8):
                            ins = e.matmul(psA[:, col:col + 1], lhsT=pb[:, k, m * 128:(m + 1) * 128], rhs=cond[:, k:k + 1],
                                           start=(k == 0), stop=(k == 7))
                    return ins
                P.op("pe", mm, r=[pb, cond], w=[psA])
        mc = G.modc
        mflat = lambda: mc[:].rearrange("p l j k -> p (l j k)")
        P.op("dve", lambda e: e.tensor_tensor(out=mflat(), in0=psA[:, 0:depth * 72], in1=adab[:].rearrange("p l c -> p (l c)"),
                                              op=ALU.add), r=[psA, adab], w=[mc])
        for j in (1, 4, 5, 7):
            P.op("dve", lambda e, j=j: e.tensor_scalar(out=mc[:, :, j, :], in0=mc[:, :, j, :], scalar1=1.0, scalar2=None, op0=ALU.add),
                 r=[mc], w=[mc])
        for j in (2, 8):
            P.op("dve", lambda e, j=j: e.tensor_scalar(out=mc[:, :, j, :], in0=mc[:, :, j, :], scalar1=0.5, scalar2=0.5,
                                                       op0=ALU.mult, op1=ALU.add), r=[mc], w=[mc])
        lb = P.sb([128, depth, 8], F32)
        ex = P.sb([128, depth, 8], F32)
        mx = P.sb([128, 8], F32)
        sm = P.sb([128, 8], F32)
        cs = P.sb([128, 8], F32)
        ld(lb[:], I["lbc"].rearrange("l p c -> p l c"), [lb])
        P.op("dve", lambda e: e.tensor_copy(out=mx[:], in_=lb[:, 0, :]), r=[lb], w=[mx])
        for l in range(1, depth):
            P.op("dve", lambda e, l=l: e.tensor_tensor(out=mx[:], in0=mx[:], in1=lb[:, l, :], op=ALU.max), r=[lb, mx], w=[mx])
        for l in range(depth):
            P.op("dve", lambda e, l=l: e.tensor_tensor(out=ex[:, l, :], in0=lb[:, l, :], in1=mx[:], op=ALU.subtract), r=[lb, mx], w=[ex])
        P.op("act", lambda e: e.activation(out=ex[:], in_=ex[:], func=AF.Exp), r=[ex], w=[ex])
        P.op("dve", lambda e: e.tensor_copy(out=sm[:], in_=ex[:, 0, :]), r=[ex], w=[sm])
        for l in range(1, depth):
            P.op("dve", lambda e, l=l: e.tensor_tensor(out=sm[:], in0=sm[:], in1=ex[:, l, :], op=ALU.add), r=[ex, sm], w=[sm])
        P.op("dve", lambda e: e.reciprocal(out=sm[:], in_=sm[:]), r=[sm], w=[sm])
        for l in range(depth):
            P.op("dve", lambda e, l=l: e.tensor_tensor(out=ex[:, l, :], in0=ex[:, l, :], in1=sm[:], op=ALU.mult), r=[ex, sm], w=[ex])
        P.op("dve", lambda e: e.tensor_copy(out=cs[:], in_=ex[:, 0, :]), r=[ex], w=[cs])
        om = G.omlc
        for l in range(depth):
            if l > 0:
                P.op("dve", lambda e, l=l: e.tensor_tensor(out=cs[:], in0=cs[:], in1=ex[:, l, :], op=ALU.add), r=[ex, cs], w=[cs])
            P.op("dve", lambda e, l=l: e.tensor_tensor(out=om[:, l, :], in0=cs[:], in1=ex[:, 0, :], op=ALU.subtract), r=[ex, cs], w=[om])
        P.op("dve", lambda e: e.tensor_scalar(out=om[:], in0=om[:], scalar1=-1.0, scalar2=1.0, op0=ALU.mult, op1=ALU.add), r=[om], w=[om])
        inv = P.sb([64, 1], F32)
        ld(inv[:], I["c_inv2"], [inv])
        TC = min(T, 2048)
        pi_ = P.sb([64, TC], I32)
        ang = P.sb([64, TC], F32)
        angc = P.sb([64, TC], F32)
        tmp = P.sb([64, TC], F32)
        ki = P.sb([64, TC], I32)
        kf = P.sb([64, TC], F32)
        tab = P.sb([64, TC], F32)
        tab2 = P.sb([64, TC], F32)
        C1 = 6.28125
        C2 = 2 * math.pi - C1
        PI = math.pi

        def reduce_angle(src, half):
            P.op("dve", lambda e: e.tensor_scalar(out=tmp[:], in0=src[:], scalar1=1.0 / (2 * PI), scalar2=half, op0=ALU.mult, op1=ALU.add),
                 r=[src], w=[tmp])
            P.op("dve", lambda e: e.tensor_copy(out=ki[:], in_=tmp[:]), r=[tmp], w=[ki])
            P.op("dve", lambda e: e.tensor_copy(out=kf[:], in_=ki[:]), r=[ki], w=[kf])
            P.op("dve", lambda e: e.scalar_tensor_tensor(out=tmp[:], in0=kf[:], scalar=-C1, in1=src[:], op0=ALU.mult, op1=ALU.add),
                 r=[kf, src], w=[tmp])
            P.op("dve", lambda e: e.scalar_tensor_tensor(out=tmp[:], in0=kf[:], scalar=-C2, in1=tmp[:], op0=ALU.mult, op1=ALU.add),
                 r=[kf, tmp], w=[tmp])
            P.op("dve", lambda e: e.tensor_scalar(out=kf[:], in0=tmp[:], scalar1=PI, scalar2=-2 * PI, op0=ALU.is_gt, op1=ALU.mult),
                 r=[tmp], w=[kf])
            P.op("dve", lambda e: e.tensor_tensor(out=tmp[:], in0=tmp[:], in1=kf[:], op=ALU.add), r=[tmp, kf], w=[tmp])
            P.op("dve", lambda e: e.tensor_scalar(out=kf[:], in0=tmp[:], scalar1=-PI, scalar2=2 * PI, op0=ALU.is_lt, op1=ALU.mult),
                 r=[tmp], w=[kf])
            P.op("dve", lambda e: e.tensor_tensor(out=tmp[:], in0=tmp[:], in1=kf[:], op=ALU.add), r=[tmp, kf], w=[tmp])
            P.op("dve", lambda e: e.tensor_scalar(out=tmp[:], in0=tmp[:], scalar1=-PI, scalar2=PI, op0=ALU.max, op1=ALU.min),
                 r=[tmp], w=[tmp])

        for t0 in range(0, T, TC):
            P.dma("sp", lambda e, t0=t0: e.dma_start(out=pi_[:], in_=I["pos"][:, t0:t0 + TC].partition_broadcast(64)), w=[pi_])
            P.op("dve", lambda e: e.tensor_copy(out=ang[:], in_=pi_[:]), r=[pi_], w=[ang])
            P.op("dve", lambda e: e.tensor_scalar(out=ang[:], in0=ang[:], scalar1=inv[:, 0:1], scalar2=None, op0=ALU.mult),
                 r=[ang, inv], w=[ang])
            reduce_angle(ang, 0.5)
            P.op("act", lambda e: e.activation(out=tab2[0:32, :], in_=tmp[0:32, :], func=AF.Sin, scale=-1.0), r=[tmp], w=[tab2])
            P.op("act", lambda e: e.activation(out=tab2[32:64, :], in_=tmp[32:64, :], func=AF.Sin), r=[tmp], w=[tab2])
            P.dma("sp", lambda e, t0=t0: e.dma_start(out=G.sinS[:, t0:t0 + TC], in_=tab2[:]), r=[tab2])
            P.op("dve", lambda e: e.tensor_scalar(out=angc[:], in0=tmp[:], scalar1=0.5 * PI, scalar2=None, op0=ALU.add), r=[tmp], w=[angc])
            P.op("dve", lambda e: e.tensor_scalar(out=kf[:], in0=angc[:], scalar1=PI, scalar2=-2 * PI, op0=ALU.is_gt, op1=ALU.mult),
                 r=[angc], w=[kf])
            P.op("dve", lambda e: e.tensor_tensor(out=angc[:], in0=angc[:], in1=kf[:], op=ALU.add), r=[angc, kf], w=[angc])
            P.op("dve", lambda e: e.tensor_scalar(out=angc[:], in0=angc[:], scalar1=-PI, scalar2=PI, op0=ALU.max, op1=ALU.min),
                 r=[angc], w=[angc])
            P.op("act", lambda e: e.activation(out=tab[:], in_=angc[:], func=AF.Sin), r=[angc], w=[tab])
            P.dma("sp", lambda e, t0=t0: e.dma_start(out=G.cosS[:, t0:t0 + TC], in_=tab[:]), r=[tab])

def layer_norm_tile(P, G, l, lnidx, rch, ps_s, ps_q, sqb, st):
    ones = G.ones_f
    for m in range(8):
        ap, rs = rch[m]
        sb_ = sqb[m % 2]
        P.op("act", lambda e, ap=ap, sb_=sb_: e.activation(out=sb_[:], in_=ap(), func=AF.Square), r=[rs], w=[sb_])
        P.op("pe", lambda e, ap=ap, m=m: e.matmul(ps_s[:], lhsT=ones[:], rhs=ap(), start=(m == 0), stop=(m == 7)), r=[rs, ones], w=[ps_s])
        P.op("pe", lambda e, sb_=sb_, m=m: e.matmul(ps_q[:], lhsT=ones[:], rhs=sb_[:], start=(m == 0), stop=(m == 7)), r=[sb_, ones], w=[ps_q])
    mean, msq, rstd = st["mean"], st["msq"], st["rstd"]
    P.op("dve", lambda e: e.tensor_scalar(out=mean[:], in0=ps_s[:], scalar1=1.0 / D, scalar2=None, op0=ALU.mult), r=[ps_s], w=[mean])
    P.op("dve", lambda e: e.tensor_tensor(out=msq[:], in0=mean[:], in1=mean[:], op=ALU.mult), r=[mean], w=[msq])
    P.op("dve", lambda e: e.scalar_tensor_tensor(out=msq[:], in0=ps_q[:], scalar=1.0 / D, in1=msq[:], op0=ALU.mult, op1=ALU.subtract),
         r=[ps_q, msq], w=[msq])
    P.op("act", lambda e: e.activation(out=rstd[:], in_=msq[:], func=AF.Sqrt, bias=LN_EPS, scale=1.0), r=[msq], w=[rstd])
    P.op("dve", lambda e: e.reciprocal(out=rstd[:], in_=rstd[:]), r=[rstd], w=[rstd])
    for m in range(8):
        ap, rs = rch[m]
        P.op("dve", lambda e, ap=ap: e.tensor_tensor(out=ap(), in0=ap(), in1=mean[:], op=ALU.subtract), r=[rs, mean], w=[rs])
        P.op("dve", lambda e, ap=ap: e.tensor_tensor(out=ap(), in0=ap(), in1=rstd[:], op=ALU.mult), r=[rs, rstd], w=[rs])
        c = lnidx * 8 + m
        P.op("act", lambda e, ap=ap, c=c: e.activation(out=ap(), in_=ap(), func=AF.Identity, bias=G.lnb[:, l, c:c + 1],
                                                        scale=G.lng[:, l, c:c + 1]), r=[rs], w=[rs])


def phase_ffn(G, l, f, hin, hout):
    T = G.T
    NS = min(2, T // NT)
    ST = NS * NT
    nst = T // ST
    ib, isc, ig = (0, 1, 2) if f == 1 else (6, 7, 8)
    lnidx = 0 if f == 1 else 2
    mc = G.modc
    with Phase(G, "f%d_%d" % (f, l)) as P:
        wgu = [P.sb([128, 2, 8, 128], BF16) for _ in range(3)]
        wdm = [P.sb([128, NJ, 128], BF16) for _ in range(3)]
        hb = [P.sb([128, 8, ST], F32) for _ in range(2)]
        hR = [[[Res() for _ in range(NS)] for _ in range(8)] for _ in range(2)]
        ub = [P.sb([128, 8, ST], BF16) for _ in range(2)]
        hid = P.sb([128, NJ, ST], BF16)
        hidR = [[Res() for _ in range(NS)] for _ in range(NJ)]
        sg = [P.sb([128, NT], F32) for _ in range(2)]
        sqb = [P.sb([128, NT], F32) for _ in range(2)]
        st = {k: P.sb([128, NT], F32) for k in ("mean", "msq", "rstd")}
        psG, psU, psY, ps_s, ps_q = G.ps[0:2], G.ps[2:4], G.ps[4:6], G.ps[6], G.ps[7]
        allR = lambda slot: [hR[slot][m][s] for m in range(8) for s in range(NS)]

        def load_h(sti):
            slot = sti % 2
            P.dma("sp", lambda e: e.dma_start(out=hb[slot][:], in_=hin[:, sti * ST:(sti + 1) * ST].rearrange("(k p) t -> p k t", p=128)),
                  w=allR(slot))

        nwl = [0]

        def load_wgu(j):
            b = wgu[nwl[0] % 3]
            nwl[0] += 1
            P.dma("sp", lambda e: e.dma_start(out=b[:], in_=G.WguS[l][f - 1][j]), w=[b])
            return b

        nwd = [0]

        def load_wd(m):
            b = wdm[nwd[0] % 3]
            nwd[0] += 1
            P.dma("sp", lambda e: e.dma_start(out=b[:], in_=G.WdS[l][f - 1][m]), w=[b])
            return b

        load_h(0)
        cnt = [0]
        def do_tile(sti):
            slot = sti % 2
            h, u = hb[slot], ub[slot]
            if sti + 1 < nst:
                load_h(sti + 1)
            wq = [load_wgu(0), load_wgu(1)]
            for m in range(8):
                for s in range(NS):
                    sl = slice(s * NT, (s + 1) * NT)
                    P.op("act", lambda e, m=m, sl=sl: e.activation(out=u[:, m, sl], in_=h[:, m, sl], func=AF.Identity,
                                                                 bias=mc[:, l, ib, m:m + 1], scale=mc[:, l, isc, m:m + 1]),
                         r=[hR[slot][m][s]], w=[u])
            for m in range(8):
                for s in range(NS):
                    sl = slice(s * NT, (s + 1) * NT)
                    P.op("dve", lambda e, m=m, sl=sl: e.tensor_scalar(out=h[:, m, sl], in0=h[:, m, sl], scalar1=ALPHA, scalar2=None,
                                                                      op0=ALU.mult), r=[hR[slot][m][s]], w=[hR[slot][m][s]])
            for j in range(NJ):
                w = wq.pop(0)
                if j + 2 < NJ:
                    wq.append(load_wgu(j + 2))
                for s in range(NS):
                    sl = slice(s * NT, (s + 1) * NT)
                    pg, pu, sgb = psG[cnt[0] % 2], psU[cnt[0] % 2], sg[cnt[0] % 2]
                    cnt[0] += 1
                    for g, pt in ((0, pg), (1, pu)):
                        def mm(e, g=g, pt=pt, w=w, sl=sl):
                            for k in range(8):
                                ins = e.matmul(pt[:], lhsT=w[:, g, k, :], rhs=u[:, k, sl], start=(k == 0), stop=(k == 7))
                            return ins
                        P.op("pe", mm, r=[w, u], w=[pt])
                    P.op("act", lambda e, pg=pg, sgb=sgb: e.activation(out=sgb[:], in_=pg[:], func=AF.Silu), r=[pg], w=[sgb])
                    P.op("dve", lambda e, pu=pu, sgb=sgb, j=j, sl=sl: e.tensor_tensor(out=hid[:, j, sl], in0=pu[:], in1=sgb[:], op=ALU.mult),
                         r=[pu, sgb], w=[hidR[j][s]])
            wdq = [load_wd(0), load_wd(1)]
            for m in range(8):
                w = wdq.pop(0)
                if m + 2 < 8:
                    wdq.append(load_wd(m + 2))
                for s in range(NS):
                    sl = slice(s * NT, (s + 1) * NT)
                    py = psY[cnt[0] % 2]
                    cnt[0] += 1

                    def mmd(e, py=py, w=w, sl=sl):
                        for j in range(NJ):
                            ins = e.matmul(py[:], lhsT=w[:, j, :], rhs=hid[:, j, sl], start=(j == 0), stop=(j == NJ - 1))
                        return ins
                    P.op("pe", mmd, r=[w] + [hidR[j][s] for j in range(NJ)], w=[py])
                    P.op("dve", lambda e, py=py, m=m, sl=sl: e.scalar_tensor_tensor(
                        out=h[:, m, sl], in0=py[:], scalar=mc[:, l, ig, m:m + 1], in1=h[:, m, sl], op0=ALU.mult, op1=ALU.add),
                        r=[py, hR[slot][m][s]], w=[hR[slot][m][s]])
            for s in range(NS):
                sl = slice(s * NT, (s + 1) * NT)
                rch = [((lambda m=m, sl=sl: h[:, m, sl]), hR[slot][m][s]) for m in range(8)]
                layer_norm_tile(P, G, l, lnidx, rch, ps_s, ps_q, sqb, st)
            P.dma("sp", lambda e, sti=sti, h=h: e.dma_start(out=hout[:, sti * ST:(sti + 1) * ST].rearrange("(k p) t -> p k t", p=128), in_=h[:]),
                  r=allR(slot))
        for sti in range(nst):
            do_tile(sti)


def _col(v):
    v = np.asarray(v)
    n = v.shape[-1] // 128
    return np.ascontiguousarray(np.swapaxes(v.reshape(v.shape[:-1] + (n, 128)), -1, -2))


def host_consts():
    ident = np.eye(128, dtype=np.float32).astype(ml_dtypes.bfloat16)
    tri = (np.arange(128)[:, None] <= np.arange(128)[None, :]).astype(np.float32).astype(ml_dtypes.bfloat16)
    reset = np.ones((128, NT), np.float32)
    reset[:, ::64] = 0.0
    inv = (1.0 / (np.float32(10000.0) ** (np.arange(0, 64, 2, dtype=np.float32) / np.float32(64)))).astype(np.float32)
    inv2 = np.concatenate([inv, inv]).reshape(64, 1).astype(np.float32)
    tri8 = np.ascontiguousarray(np.tile(tri[0:64, 0:64], (1, 8)))
    return {"c_ident": ident, "c_tri": tri, "c_tri8": tri8, "c_reset": reset, "c_inv2": inv2}


def prep_shared(inp, depth):
    f = lambda k: np.ascontiguousarray(np.asarray(inp[k], dtype=np.float32)[:depth])
    sh = {}
    for k in ("ada_w", "ffn1_gate", "ffn1_up", "ffn1_down", "ffn2_gate", "ffn2_up", "ffn2_down", "w_in"):
        sh[k] = f(k)
    sh["w_uq"] = f("mla_w_uq")
    sh["w_ukv"] = f("mla_w_ukv")
    sh["w_bhg"] = f("w_branch_hg")
    sh["w_bmla"] = f("w_branch_mla")
    sh["w_out"] = f("w_out")
    sh["ada_bc"] = _col(f("ada_b"))
    sh["lngc"] = _col(f("ln_g").reshape(depth, 3 * D))
    sh["lnbc"] = _col(f("ln_b").reshape(depth, 3 * D))
    sh["lbc"] = _col(f("hg_lower_bound"))
    sh["hgnc"] = _col(f("hg_norm_g").reshape(depth, D))
    sh["qnc"] = _col(f("mla_q_norm_g"))
    sh["kvnc"] = _col(f("mla_kv_norm_g"))
    sh.update(host_consts())
    return sh


def prep_core(inp, b):
    x = np.asarray(inp["x"], dtype=np.float32)[b]
    return {
        "xT": np.ascontiguousarray(x.T),
        "ccol": _col(np.asarray(inp["c"], dtype=np.float32)[b]),
        "pos": np.ascontiguousarray(np.asarray(inp["positions"]).astype(np.int32)[b].reshape(1, -1)),
    }


class Rot:
    def __init__(self, bufs):
        self.bufs = bufs
        self.i = 0

    def next(self):
        b = self.bufs[self.i % len(self.bufs)]
        self.i += 1
        return b


def _load_u(P, G, l, hin, hb, ub, ti, isc, ib):
    slot = ti % 2
    P.dma("sp", lambda e: e.dma_start(out=hb[slot][:], in_=hin[:, ti * NT:(ti + 1) * NT].rearrange("(k p) t -> p k t", p=128)),
          w=[hb[slot]])


def _make_u(P, G, l, hb, ub, ti, isc, ib):
    slot = ti % 2
    h, u = hb[slot], ub[slot]
    mc = G.modc
    for m in range(8):
        P.op("act", lambda e, m=m: e.activation(out=u[:, m, :], in_=h[:, m, :], func=AF.Identity, bias=mc[:, l, ib, m:m + 1],
                                                scale=mc[:, l, isc, m:m + 1]), r=[h], w=[u])
    return u


def phase_m1a(G, l, hin):
    T = G.T
    ntile = T // NT
    with Phase(G, "m1a_%d" % l) as P:
        wres = P.sb([128, 24, 8, 128], BF16)
        wR = [Res() for _ in range(24)]
        wzi = P.sb([128, 8, D], BF16)
        hb = [P.sb([128, 8, NT], F32) for _ in range(2)]
        ub = [P.sb([128, 8, NT], BF16) for _ in range(2)]
        tmp = [{k: P.sb([128, NT], F32) for k in ("hk", "lf", "G", "D1", "E1", "E2", "hq")} for _ in range(2)]
        abt = [P.sb([128, 3, 8], F32) for _ in range(2)]
        stage = Rot([P.sb([128, NT], BF16) for _ in range(6)])
        vst = Rot([P.sb([128, D], BF16) for _ in range(2)])
        psr = Rot(G.ps)
        _load_u(P, G, l, hin, hb, ub, 0, 4, 3)
        for h in range(8):
            for c in (8 + h, h):
                P.dma("sp", lambda e, c=c: e.dma_start(out=wres[:, c], in_=G.WinS[l][c]), w=[wR[c]])
        for c in range(16, 24):
            P.dma("sp", lambda e, c=c: e.dma_start(out=wres[:, c], in_=G.WinS[l][c]), w=[wR[c]])
        P.dma("sp", lambda e: e.dma_start(out=wzi[:], in_=G.WziS[l].rearrange("(k p) n -> p k n", p=128)), w=[wzi])

        def proj(c, u):
            ps = psr.next()

            def mm(e):
                for k in range(8):
                    ins = e.matmul(ps[:], lhsT=wres[:, c, k, :], rhs=u[:, k, :], start=(k == 0), stop=(k == 7))
                return ins
            P.op("pe", mm, r=[wR[c], u], w=[ps])
            return ps

        def do_tile(ti):
            t0 = ti * NT
            if ti + 1 < ntile:
                _load_u(P, G, l, hin, hb, ub, ti + 1, 4, 3)
            u = _make_u(P, G, l, hb, ub, ti, 4, 3)
            for h in range(8):
                t = tmp[h % 2]
                hk, lf, Gt, D1, E1, E2, hq = (t[k] for k in ("hk", "lf", "G", "D1", "E1", "E2", "hq"))
                ab = abt[h % 2]
                pz = proj(8 + h, u)
                P.op("act", lambda e, pz=pz, hk=hk: e.activation(out=hk[:], in_=pz[:], func=AF.Sigmoid, scale=-1.0), r=[pz], w=[hk])
                P.op("dve", lambda e, hk=hk, h=h: e.tensor_scalar(out=hk[:], in0=hk[:], scalar1=G.omlc[:, l, h:h + 1], scalar2=None,
                                                                  op0=ALU.mult), r=[hk], w=[hk])
                P.op("act", lambda e, hk=hk, lf=lf: e.activation(out=lf[:], in_=hk[:], func=AF.Ln, scale=-1.0, bias=1.0), r=[hk], w=[lf])
                P.op("dve", lambda e, lf=lf, Gt=Gt: e.tensor_tensor_scan(out=Gt[:], data0=G.reset[:], data1=lf[:], initial=0.0,
                                                                         op0=ALU.mult, op1=ALU.add), r=[lf], w=[Gt])
                gv = lambda Gt=Gt: Gt[:].rearrange("p (c j) -> p c j", j=64)
                dv = lambda D1=D1: D1[:].rearrange("p (c j) -> p c j", j=64)
                P.op("dve", lambda e, gv=gv, dv=dv: e.tensor_tensor(out=dv(), in0=gv(), in1=gv()[:, :, 31:32].to_broadcast([128, 8, 64]),
                                                                    op=ALU.subtract), r=[Gt], w=[D1])
                P.op("dve", lambda e, D1=D1: e.tensor_scalar(out=D1[:], in0=D1[:], scalar1=-40.0, scalar2=40.0, op0=ALU.max, op1=ALU.min),
                     r=[D1], w=[D1])
                P.op("act", lambda e, D1=D1, E1=E1: e.activation(out=E1[:], in_=D1[:], func=AF.Exp), r=[D1], w=[E1])
                P.op("act", lambda e, D1=D1, E2=E2: e.activation(out=E2[:], in_=D1[:], func=AF.Exp, scale=-1.0), r=[D1], w=[E2])
                P.op("act", lambda e, gv=gv, ab=ab: e.activation(out=ab[:, 0, :], in_=gv()[:, :, 31], func=AF.Exp), r=[Gt], w=[ab])
                ev = lambda E1=E1: E1[:].rearrange("p (c j) -> p c j", j=64)
                P.op("dve", lambda e, ev=ev, ab=ab: e.tensor_copy(out=ab[:, 1, :], in_=ev()[:, :, 63]), r=[E1], w=[ab])
                P.op("dve", lambda e, ab=ab: e.tensor_tensor(out=ab[:, 2, :], in0=ab[:, 0, :], in1=ab[:, 1, :], op=ALU.mult), r=[ab], w=[ab])
                P.dma("sp", lambda e, ab=ab, h=h, ti=ti: e.dma_start(out=G.abS[ti][:, h], in_=ab[:]), r=[ab])
                pq = proj(h, u)
                P.op("act", lambda e, pq=pq, hq=hq: e.activation(out=hq[:], in_=pq[:], func=AF.Silu), r=[pq], w=[hq])
                qs, ks = stage.next(), stage.next()
                P.op("dve", lambda e, qs=qs, hq=hq, E1=E1: e.tensor_tensor(out=qs[:], in0=hq[:], in1=E1[:], op=ALU.mult), r=[hq, E1], w=[qs])
                P.dma("sp", lambda e, qs=qs, h=h, t0=t0: e.dma_start(out=G.qtT[h][:, t0:t0 + NT], in_=qs[:]), r=[qs])
                P.op("dve", lambda e, ks=ks, hk=hk, E2=E2: e.tensor_tensor(out=ks[:], in0=hk[:], in1=E2[:], op=ALU.mult), r=[hk, E2], w=[ks])
                P.dma("sp", lambda e, ks=ks, h=h, t0=t0: e.dma_start(out=G.ktT[h][:, t0:t0 + NT], in_=ks[:]), r=[ks])
            for j in range(8):
                pg = proj(16 + j, u)
                zs = stage.next()
                P.op("act", lambda e, pg=pg, zs=zs: e.activation(out=zs[:], in_=pg[:], func=AF.Silu), r=[pg], w=[zs])
                P.dma("sp", lambda e, zs=zs, j=j, t0=t0: e.dma_start(out=G.zgT[j * 128:(j + 1) * 128, t0:t0 + NT], in_=zs[:]), r=[zs])
            for tb in range(4):
                vs = vst.next()
                for half in range(2):
                    ps = psr.next()

                    def mmv(e, ps=ps, tb=tb, half=half):
                        for k in range(8):
                            ins = e.matmul(ps[:], lhsT=u[:, k, tb * 128:(tb + 1) * 128], rhs=wzi[:, k, half * 512:(half + 1) * 512],
                                           start=(k == 0), stop=(k == 7))
                        return ins
                    P.op("pe", mmv, r=[wzi, u], w=[ps])
                    if half == 0:
                        P.op("act", lambda e, ps=ps, vs=vs: e.activation(out=vs[:, 0:512], in_=ps[:], func=AF.Copy), r=[ps], w=[vs])
                    else:
                        P.op("dve", lambda e, ps=ps, vs=vs: e.tensor_copy(out=vs[:, 512:1024], in_=ps[:]), r=[ps], w=[vs])
                P.dma("sp", lambda e, vs=vs, tb=tb, t0=t0: e.dma_start(out=G.vhg[t0 + tb * 128:t0 + (tb + 1) * 128, :], in_=vs[:]), r=[vs])
        for ti in range(ntile):
            do_tile(ti)


def phase_m1b(G, l, hin):
    T = G.T
    ntile = T // NT
    with Phase(G, "m1b_%d" % l) as P:
        wres = P.sb([128, 23, 8, 128], BF16)
        wR = [Res() for _ in range(23)]
        wuq = P.sb([128, 24, 3, 128], BF16)
        wukn = P.sb([128, 8, 2, 128], BF16)
        wukv = P.sb([128, 2, D], BF16)
        hb = [P.sb([128, 8, NT], F32) for _ in range(2)]
        ub = [P.sb([128, 8, NT], BF16) for _ in range(2)]
        cs = [P.sb([64, 2, NT], F32) for _ in range(2)]
        cq = P.sb([128, 3, NT], F32)
        cqn = P.sb([128, 3, NT], BF16)
        ckv = P.sb([128, 2, NT], F32)
        ckvn = P.sb([128, 2, NT], BF16)
        sqb = Rot([P.sb([128, NT], F32) for _ in range(3)])
        rstd = Rot([P.sb([128, NT], F32) for _ in range(2)])
        rt = Rot([P.sb([64, NT], F32) for _ in range(4)])
        stage = Rot([P.sb([128, NT], BF16) for _ in range(6)])
        vst = Rot([P.sb([128, D], BF16) for _ in range(2)])
        psr = Rot(G.ps)
        _load_u(P, G, l, hin, hb, ub, 0, 4, 3)
        for i, c in enumerate(list(range(24, 29)) + [45, 46] + list(range(29, 45))):
            dst = {45: 21, 46: 22}.get(c, c - 24)
            P.dma("sp", lambda e, c=c, dst=dst: e.dma_start(out=wres[:, dst], in_=G.WinS[l][c]), w=[wR[dst]])
        P.dma("sp", lambda e: e.dma_start(out=wuq[:], in_=G.WuqS[l].rearrange("s p k m -> p s k m")), w=[wuq])
        P.dma("sp", lambda e: e.dma_start(out=wukn[:], in_=G.WuknS[l].rearrange("s p k m -> p s k m")), w=[wukn])
        P.dma("sp", lambda e: e.dma_start(out=wukv[:], in_=G.WukvV[l].rearrange("(k p) n -> p k n", p=128)), w=[wukv])

        def proj(c, u, mrows=128):
            ps = psr.next()

            def mm(e):
                for k in range(8):
                    ins = e.matmul(ps[0:mrows, :], lhsT=wres[:, c, k, 0:mrows], rhs=u[:, k, :], start=(k == 0), stop=(k == 7))
                return ins
            P.op("pe", mm, r=[wR[c], u], w=[ps])
            return ps

        def rmsnorm(u, chunks, nk, raw, normed, gcol):
            sqs = []
            for k in range(nk):
                pk = proj(chunks[k], u)
                sq = sqb.next()
                sqs.append(sq)
                P.op("dve", lambda e, pk=pk, k=k: e.tensor_copy(out=raw[:, k, :], in_=pk[:]), r=[pk], w=[raw])
                P.op("act", lambda e, k=k, sq=sq: e.activation(out=sq[:], in_=raw[:, k, :], func=AF.Square), r=[raw], w=[sq])
            psn = psr.next()

            def mmn(e):
                for k in range(nk):
                    ins = e.matmul(psn[:], lhsT=G.ones_f[:], rhs=sqs[k][:], start=(k == 0), stop=(k == nk - 1))
                return ins
            P.op("pe", mmn, r=sqs + [G.ones_f], w=[psn])
            rs = rstd.next()
            P.op("act", lambda e: e.activation(out=rs[:], in_=psn[:], func=AF.Sqrt, bias=RMS_EPS, scale=1.0 / (128 * nk)), r=[psn], w=[rs])
            P.op("dve", lambda e: e.reciprocal(out=rs[:], in_=rs[:]), r=[rs], w=[rs])
            for k in range(nk):
                P.op("dve", lambda e, k=k: e.scalar_tensor_tensor(out=normed[:, k, :], in0=raw[:, k, :], scalar=gcol(k), in1=rs[:],
                                                                  op0=ALU.mult, op1=ALU.mult), r=[raw, rs], w=[normed])

        def rope(psA, psB, cst, out_ap, out_buf):
            t1, t2 = rt.next(), rt.next()
            P.op("dve", lambda e: e.tensor_tensor(out=t1[:], in0=psA[0:64, :], in1=cst[:, 0, :], op=ALU.mult), r=[psA, cst], w=[t1])
            P.op("dve", lambda e: e.tensor_tensor(out=t2[:], in0=psB[0:64, :], in1=cst[:, 1, :], op=ALU.mult), r=[psB, cst], w=[t2])
            P.op("dve", lambda e: e.tensor_tensor(out=out_ap(), in0=t1[:], in1=t2[:], op=ALU.add), r=[t1, t2], w=[out_buf])

        def load_cs(ti):
            c = cs[ti % 2]
            P.dma("sp", lambda e: e.dma_start(out=c[:, 0, :], in_=G.cosS[:, ti * NT:(ti + 1) * NT]), w=[c])
            P.dma("sp", lambda e: e.dma_start(out=c[:, 1, :], in_=G.sinS[:, ti * NT:(ti + 1) * NT]), w=[c])

        load_cs(0)
        def do_tile(ti):
            t0 = ti * NT
            if ti + 1 < ntile:
                _load_u(P, G, l, hin, hb, ub, ti + 1, 4, 3)
                load_cs(ti + 1)
            u = _make_u(P, G, l, hb, ub, ti, 4, 3)
            cst = cs[ti % 2]
            import os
            MS = int(os.environ.get("M1B_STOP", "9"))
            if MS <= 0:
                return
            rmsnorm(u, [0, 1, 2], 3, cq, cqn, lambda k: G.qn[:, l, k:k + 1])
            if MS <= 1:
                return
            for h in range(NH):
                ps = psr.next()

                def mmq(e, ps=ps, h=h):
                    for k in range(3):
                        ins = e.matmul(ps[:], lhsT=wuq[:, 3 * h, k, :], rhs=cqn[:, k, :], start=(k == 0), stop=(k == 2))
                    return ins
                P.op("pe", mmq, r=[wuq, cqn], w=[ps])
                st = stage.next()
                P.op("act", lambda e, ps=ps, st=st: e.activation(out=st[:], in_=ps[:], func=AF.Copy), r=[ps], w=[st])
                P.dma("sp", lambda e, st=st, h=h, t0=t0: e.dma_start(out=G.qnT[h][:, t0:t0 + NT], in_=st[:]), r=[st])
                pab = []
                for v in (1, 2):
                    ps = psr.next()

                    def mmr(e, ps=ps, h=h, v=v):
                        for k in range(3):
                            ins = e.matmul(ps[0:64, :], lhsT=wuq[:, 3 * h + v, k, 0:64], rhs=cqn[:, k, :], start=(k == 0), stop=(k == 2))
                        return ins
                    P.op("pe", mmr, r=[wuq, cqn], w=[ps])
                    pab.append(ps)
                st = stage.next()
                rope(pab[0], pab[1], cst, lambda st=st: st[0:64, :], st)
                P.dma("sp", lambda e, st=st, h=h, t0=t0: e.dma_start(out=G.qrT[h][:, t0:t0 + NT], in_=st[0:64, :]), r=[st])
            if MS <= 2:
                return
            rmsnorm(u, [3, 4], 2, ckv, ckvn, lambda k: G.kvn[:, l, k:k + 1])
            for h in range(NH):
                ps = psr.next()

                def mmk(e, ps=ps, h=h):
                    for k in range(2):
                        ins = e.matmul(ps[:], lhsT=wukn[:, h, k, :], rhs=ckvn[:, k, :], start=(k == 0), stop=(k == 1))
                    return ins
                P.op("pe", mmk, r=[wukn, ckvn], w=[ps])
                st = stage.next()
                if h % 2 == 0:
                    P.op("act", lambda e, ps=ps, st=st: e.activation(out=st[:], in_=ps[:], func=AF.Copy), r=[ps], w=[st])
                else:
                    P.op("dve", lambda e, ps=ps, st=st: e.tensor_copy(out=st[:], in_=ps[:]), r=[ps], w=[st])
                P.dma("sp", lambda e, st=st, h=h, t0=t0: e.dma_start(out=G.knT[h][:, t0:t0 + NT], in_=st[:]), r=[st])
            for tb in range(4):
                vs = vst.next()
                for half in range(2):
                    ps = psr.next()

                    def mmv(e, ps=ps, tb=tb, half=half):
                        for k in range(2):
                            ins = e.matmul(ps[:], lhsT=ckvn[:, k, tb * 128:(tb + 1) * 128], rhs=wukv[:, k, half * 512:(half + 1) * 512],
                                           start=(k == 0), stop=(k == 1))
                        return ins
                    P.op("pe", mmv, r=[wukv, ckvn], w=[ps])
                    if half == 0:
                        P.op("act", lambda e, ps=ps, vs=vs: e.activation(out=vs[:, 0:512], in_=ps[:], func=AF.Copy), r=[ps], w=[vs])
                    else:
                        P.op("dve", lambda e, ps=ps, vs=vs: e.tensor_copy(out=vs[:, 512:1024], in_=ps[:]), r=[ps], w=[vs])
                P.dma("sp", lambda e, vs=vs, tb=tb, ti=ti: e.dma_start(out=G.vm[:, :, ti * 4 + tb, :].rearrange("h p d -> p h d"),
                                                                     in_=vs[:].rearrange("p (h d) -> p h d", h=NH)), r=[vs])
            if MS <= 3:
                return
            pA = proj(21, u, 64)
            pB = proj(22, u, 64)
            st = stage.next()
            rope(pA, pB, cst, lambda st=st: st[0:64, :], st)
            P.dma("sp", lambda e, st=st, t0=t0: e.dma_start(out=G.krT[:, t0:t0 + NT], in_=st[0:64, :]), r=[st])
            if MS <= 4:
                return
            for gi, (c0, dst) in enumerate(((5, G.gaT), (13, G.gbT))):
                for j in range(8):
                    pg = proj(c0 + j, u)
                    st = stage.next()
                    P.op("act", lambda e, pg=pg, st=st: e.activation(out=st[:], in_=pg[:], func=AF.Sigmoid), r=[pg], w=[st])
                    P.dma("sp", lambda e, st=st, j=j, t0=t0, dst=dst: e.dma_start(out=dst[j * 128:(j + 1) * 128, t0:t0 + NT], in_=st[:]), r=[st])
        for ti in range(ntile):
            do_tile(ti)


def phase_hg(G, l):
    T = G.T
    ntile = T // NT
    with Phase(G, "hg_%d" % l) as P:
        wb = P.sb([128, 8, D], BF16)
        S = P.sb([128, NH, 128], F32)
        SR = [Res() for _ in range(NH)]
        qt = [P.sb([128, NH, NT], BF16) for _ in range(2)]
        kt = [P.sb([128, NH, NT], BF16) for _ in range(2)]
        vh = [P.sb([64, 8, D], BF16) for _ in range(2)]
        ab = [P.sb([128, NH, 3, 8], F32) for _ in range(2)]
        zg = [P.sb([128, 8, NT], BF16) for _ in range(2)]
        ga = [P.sb([128, 8, NT], BF16) for _ in range(2)]
        sm8 = Rot([P.sb([64, NT], BF16) for _ in range(2)])
        kT8 = Rot([P.sb([64, 8, 128], BF16) for _ in range(2)])
        sbf = Rot([P.sb([128, 128], BF16) for _ in range(4)])
        t2 = Rot([P.sb([128, 128], F32) for _ in range(3)])
        sq = P.sb([128, NT], F32)
        rstd = P.sb([128, NT], F32)
        tn = P.sb([128, NT], F32)
        og = P.sb([128, NH, NT], BF16)
        ogR = [Res() for _ in range(NH)]
        mst = Rot([P.sb([128, NT], F32) for _ in range(2)])
        poR, sR, kR = Rot([G.ps[0], G.ps[7]]), Rot(G.ps[1:2]), Rot(G.ps[2:3])
        kvR = Rot([(G.ps[3], G.ps[4]), (G.ps[5], G.ps[6])])
        psn = G.ps[1]
        yR = Rot([G.ps[1], G.ps[2]])
        P.dma("sp", lambda e: e.dma_start(out=wb[:], in_=G.WbhgS[l].rearrange("(k p) n -> p k n", p=128)), w=[wb])
        P.op("dve", lambda e: e.memset(S[:], 0.0), w=SR)

        def load(ti):
            s_ = ti % 2
            t0 = ti * NT
            P.dma("sp", lambda e: e.dma_start(out=qt[s_][:], in_=G.qtT[:, :, t0:t0 + NT].rearrange("h p t -> p h t")), w=[qt[s_]])
            P.dma("sp", lambda e: e.dma_start(out=kt[s_][:], in_=G.ktT[:, :, t0:t0 + NT].rearrange("h p t -> p h t")), w=[kt[s_]])
            P.dma("sp", lambda e: e.dma_start(out=vh[s_][:], in_=G.vhg[t0:t0 + NT, :].rearrange("(c s) d -> s c d", s=64)), w=[vh[s_]])
            P.dma("sp", lambda e: e.dma_start(out=ab[s_][:], in_=G.abS[ti]), w=[ab[s_]])
            P.dma("sp", lambda e: e.dma_start(out=zg[s_][:], in_=G.zgT[:, t0:t0 + NT].rearrange("(k p) t -> p k t", p=128)), w=[zg[s_]])
            P.dma("sp", lambda e: e.dma_start(out=ga[s_][:], in_=G.gaT[:, t0:t0 + NT].rearrange("(k p) t -> p k t", p=128)), w=[ga[s_]])

        def do_tile(ti):
            t0 = ti * NT
            s_ = ti % 2
            if ti + 1 < ntile:
                load(ti + 1)
            q_, k_, v_, a_, z_, g_ = qt[s_], kt[s_], vh[s_], ab[s_], zg[s_], ga[s_]

            def pre(h):
                pS = sR.next()

                def mms(e):
                    for c in range(8):
                        cs = slice(c * 64, (c + 1) * 64)
                        ins = e.matmul(pS[0:64, cs], lhsT=k_[:, h, cs], rhs=q_[:, h, cs], start=True, stop=True)
                    return ins
                P.op("pe", mms, r=[k_, q_], w=[pS])
                sm_ = sm8.next()
                P.op("dve", lambda e: e.tensor_tensor(out=sm_[:], in0=pS[0:64, :], in1=G.tri8[:], op=ALU.mult), r=[pS], w=[sm_])
                kT_ = kT8.next()
                for half in range(2):
                    pK = kR.next()

                    def mmk(e, pK=pK, half=half):
                        for cc in range(4):
                            c = half * 4 + cc
                            ins = e.matmul(pK[0:64, cc * 128:(cc + 1) * 128], lhsT=k_[:, h, c * 64:(c + 1) * 64], rhs=G.ident[:],
                                           start=True, stop=True)
                        return ins
                    P.op("pe", mmk, r=[k_], w=[pK])
                    P.op("act", lambda e, pK=pK, half=half: e.activation(
                        out=kT_[:, half * 4:(half + 1) * 4, :], in_=pK[0:64, :].rearrange("p (c d) -> p c d", d=128), func=AF.Copy),
                        r=[pK], w=[kT_])
                kv = kvR.next()
                for half in range(2):
                    def mmkv(e, half=half):
                        for cc in range(4):
                            c = half * 4 + cc
                            ins = e.matmul(kv[half][:, cc * 128:(cc + 1) * 128], lhsT=kT_[:, c, :], rhs=v_[0:64, c, h * 128:(h + 1) * 128],
                                           start=True, stop=True)
                        return ins
                    P.op("pe", mmkv, r=[kT_, v_], w=[kv[half]])
                return sm_, kT_, kv

            def chain(h, sm_, kT_, kv):
                po = poR.next()
                hs = slice(h * 128, (h + 1) * 128)
                sb_ = sbf.next()
                P.op("act", lambda e, sb_=sb_: e.activation(out=sb_[:], in_=S[:, h, :], func=AF.Copy, scale=a_[:, h, 0, 0:1]),
                     r=[SR[h], a_], w=[sb_])
                for c in range(8):
                    cs = slice(c * 64, (c + 1) * 64)
                    t2_ = t2.next()

                    def mmo(e, cs=cs, c=c, sb_=sb_):
                        e.matmul(po[:, cs], lhsT=v_[0:64, c, hs], rhs=sm_[:, cs], start=True, stop=False)
                        return e.matmul(po[:, cs], lhsT=sb_[:], rhs=q_[:, h, cs], start=False, stop=True)
                    P.op("pe", mmo, r=[v_, sm_, sb_, q_], w=[po])
                    P.op("dve", lambda e, t2_=t2_, c=c: e.tensor_scalar(out=t2_[:], in0=S[:, h, :], scalar1=a_[:, h, 2, c:c + 1], scalar2=None,
                                                                      op0=ALU.mult), r=[SR[h], a_], w=[t2_])
                    P.op("dve", lambda e, t2_=t2_, c=c: e.scalar_tensor_tensor(
                        out=S[:, h, :], in0=kv[c // 4][:, (c % 4) * 128:(c % 4 + 1) * 128], scalar=a_[:, h, 1, c:c + 1], in1=t2_[:],
                        op0=ALU.mult, op1=ALU.add), r=[kv[c // 4], t2_, a_], w=[SR[h]])
                    if c < 7:
                        sb_ = sbf.next()
                        P.op("act", lambda e, sb_=sb_, c=c: e.activation(out=sb_[:], in_=S[:, h, :], func=AF.Copy, scale=a_[:, h, 0, c + 1:c + 2]),
                             r=[SR[h], a_], w=[sb_])
                return po

            def norm(h, po):
                P.op("act", lambda e: e.activation(out=sq[:], in_=po[:], func=AF.Square), r=[po], w=[sq])
                P.op("pe", lambda e: e.matmul(psn[:], lhsT=G.ones_f[:], rhs=sq[:], start=True, stop=True), r=[sq], w=[psn])
                P.op("act", lambda e: e.activation(out=rstd[:], in_=psn[:], func=AF.Ln, bias=RMS_EPS, scale=1.0 / 128), r=[psn], w=[rstd])
                P.op("act", lambda e: e.activation(out=rstd[:], in_=rstd[:], func=AF.Exp, scale=-0.5), r=[rstd], w=[rstd])
                P.op("dve", lambda e: e.tensor_tensor(out=tn[:], in0=po[:], in1=rstd[:], op=ALU.mult), r=[po, rstd], w=[tn])
                P.op("dve", lambda e: e.scalar_tensor_tensor(out=og[:, h, :], in0=tn[:], scalar=G.hgn[:, l, h:h + 1], in1=z_[:, h, :],
                                                             op0=ALU.mult, op1=ALU.mult), r=[tn, z_], w=[ogR[h]])

            nxt = pre(0)
            prev = None
            for h in range(NH):
                cur = nxt
                if h + 1 < NH:
                    nxt = pre(h + 1)
                po = chain(h, *cur)
                if prev is not None:
                    norm(*prev)
                prev = (h, po)
            norm(*prev)
            for m in range(8):
                py = yR.next()

                def mmy(e, py=py, m=m):
                    for h in range(NH):
                        ins = e.matmul(py[:], lhsT=wb[:, h, m * 128:(m + 1) * 128], rhs=og[:, h, :], start=(h == 0), stop=(h == NH - 1))
                    return ins
                P.op("pe", mmy, r=[wb] + ogR, w=[py])
                ms = mst.next()
                P.op("dve", lambda e, py=py, ms=ms, m=m: e.tensor_tensor(out=ms[:], in0=py[:], in1=g_[:, m, :], op=ALU.mult), r=[py, g_], w=[ms])
                P.dma("sp", lambda e, ms=ms, m=m: e.dma_start(out=G.mhT[m * 128:(m + 1) * 128, t0:t0 + NT], in_=ms[:]), r=[ms])

        load(0)
        for ti in range(ntile):
            do_tile(ti)


def phase_a1(G, l):
    T = G.T
    nq = T // NT
    nkb = T // 128
    with Phase(G, "a1_%d" % l) as P:
        kr = P.sb([64, T], BF16)
        kn = [P.sb([128, T], BF16) for _ in range(2)]
        vv = [P.sb([128, nkb, 128], BF16) for _ in range(2)]
        qn = [P.sb([128, NT], BF16) for _ in range(2)]
        qr = [P.sb([64, NT], BF16) for _ in range(2)]
        pT = Rot([P.sb([128, NT], BF16) for _ in range(6)])
        rinv = P.sb([128, NT], F32)
        ost = Rot([P.sb([128, NT], BF16) for _ in range(2)])
        sR, oR, rR = Rot(G.ps[0:4]), Rot(G.ps[4:6]), Rot(G.ps[6:8])
        P.dma("sp", lambda e: e.dma_start(out=kr[:], in_=G.krT), w=[kr])
        cnt = [0]

        def load_head(h):
            P.dma("sp", lambda e: e.dma_start(out=kn[h % 2][:], in_=G.knT[h]), w=[kn[h % 2]])
            P.dma("sp", lambda e: e.dma_start(out=vv[h % 2][:], in_=G.vm[h]), w=[vv[h % 2]])

        def load_q(h, qi, slot):
            P.dma("sp", lambda e: e.dma_start(out=qn[slot][:], in_=G.qnT[h][:, qi * NT:(qi + 1) * NT]), w=[qn[slot]])
            P.dma("sp", lambda e: e.dma_start(out=qr[slot][:], in_=G.qrT[h][:, qi * NT:(qi + 1) * NT]), w=[qr[slot]])

        LOOK = 2

        def do_q(h, qi, slot):
            kn_, vv_, qn_, qr_ = kn[h % 2], vv[h % 2], qn[slot], qr[slot]
            po, pr = oR.next(), rR.next()
            nb = 4 * qi + 4
            pend = []

            def emit_o(kb, p_, off):
                def mmo(e):
                    e.matmul(po[:, off:NT], lhsT=vv_[:, kb, :], rhs=p_[:, off:NT], start=(kb == 0), stop=(kb == nb - 1))
                    return e.matmul(pr[:, off:NT], lhsT=G.ones_b[:], rhs=p_[:, off:NT], start=(kb == 0), stop=(kb == nb - 1))
                P.op("pe", mmo, r=[vv_, p_], w=[po, pr])

            for kb in range(nb):
                d = kb - 4 * qi
                off = 128 * d if d > 0 else 0
                ks = slice(kb * 128, (kb + 1) * 128)
                pS = sR.next()
                p_ = pT.next()

                def mms(e, pS=pS, off=off, ks=ks):
                    e.matmul(pS[:, off:NT], lhsT=kn_[:, ks], rhs=qn_[:, off:NT], start=True, stop=False)
                    return e.matmul(pS[:, off:NT], lhsT=kr[0:64, ks], rhs=qr_[0:64, off:NT], start=False, stop=True)
                P.op("pe", mms, r=[kn_, kr, qn_, qr_], w=[pS])
                P.op("act", lambda e, pS=pS, p_=p_, off=off: e.activation(out=p_[:, off:NT], in_=pS[:, off:NT], func=AF.Exp, scale=ATT_SCALE),
                     r=[pS], w=[p_])
                if d >= 0:
                    P.op("dve", lambda e, p_=p_, off=off: e.tensor_tensor(out=p_[:, off:off + 128], in0=p_[:, off:off + 128], in1=G.tri[:],
                                                                          op=ALU.mult), r=[p_], w=[p_])
                pend.append((kb, p_, off))
                if len(pend) > LOOK:
                    emit_o(*pend.pop(0))
            while pend:
                emit_o(*pend.pop(0))
            P.op("act", lambda e: e.activation(out=rinv[:], in_=pr[:], func=AF.Ln), r=[pr], w=[rinv])
            P.op("act", lambda e: e.activation(out=rinv[:], in_=rinv[:], func=AF.Exp, scale=-1.0), r=[rinv], w=[rinv])
            os_ = ost.next()
            P.op("dve", lambda e: e.tensor_tensor(out=os_[:], in0=po[:], in1=rinv[:], op=ALU.mult), r=[po, rinv], w=[os_])
            P.dma("sp", lambda e: e.dma_start(out=G.omT[h][:, qi * NT:(qi + 1) * NT], in_=os_[:]), r=[os_])

        load_head(0)
        load_q(0, 0, 0)
        seq = [(h, qi) for h in range(NH) for qi in range(nq)]
        for i, (h, qi) in enumerate(seq):
            if qi == 0 and h + 1 < NH:
                load_head(h + 1)
            if i + 1 < len(seq):
                load_q(seq[i + 1][0], seq[i + 1][1], (i + 1) % 2)
            do_q(h, qi, i % 2)


def phase_a2(G, l, hin, hout):
    T = G.T
    ntile = T // NT
    mc = G.modc
    with Phase(G, "a2_%d" % l) as P:
        wbm = P.sb([128, 8, D], BF16)
        wo = P.sb([128, 8, D], BF16)
        om = [P.sb([128, NH, NT], BF16) for _ in range(2)]
        mh = [P.sb([128, 8, NT], F32) for _ in range(2)]
        gb = [P.sb([128, 8, NT], BF16) for _ in range(2)]
        hb = [P.sb([128, 8, NT], F32) for _ in range(2)]
        hR = [[Res() for _ in range(8)] for _ in range(2)]
        mg = P.sb([128, 8, NT], BF16)
        mgR = [Res() for _ in range(8)]
        mt = Rot([P.sb([128, NT], F32) for _ in range(2)])
        sqb = [P.sb([128, NT], F32) for _ in range(2)]
        st = {k: P.sb([128, NT], F32) for k in ("mean", "msq", "rstd")}
        yR = Rot(G.ps[0:6])
        ps_s, ps_q = G.ps[6], G.ps[7]
        P.dma("sp", lambda e: e.dma_start(out=wbm[:], in_=G.WbmlaS[l].rearrange("(k p) n -> p k n", p=128)), w=[wbm])
        P.dma("sp", lambda e: e.dma_start(out=wo[:], in_=G.WoutS[l].rearrange("(k p) n -> p k n", p=128)), w=[wo])

        def load(ti):
            s_ = ti % 2
            t0 = ti * NT
            P.dma("sp", lambda e: e.dma_start(out=om[s_][:], in_=G.omT[:, :, t0:t0 + NT].rearrange("h p t -> p h t")), w=[om[s_]])
            P.dma("sp", lambda e: e.dma_start(out=mh[s_][:], in_=G.mhT[:, t0:t0 + NT].rearrange("(k p) t -> p k t", p=128)), w=[mh[s_]])
            P.dma("sp", lambda e: e.dma_start(out=gb[s_][:], in_=G.gbT[:, t0:t0 + NT].rearrange("(k p) t -> p k t", p=128)), w=[gb[s_]])
            P.dma("sp", lambda e: e.dma_start(out=hb[s_][:], in_=hin[:, t0:t0 + NT].rearrange("(k p) t -> p k t", p=128)), w=hR[s_])

        def do_tile(ti):
            s_ = ti % 2
            t0 = ti * NT
            if ti + 1 < ntile:
                load(ti + 1)
            om_, mh_, gb_, h_ = om[s_], mh[s_], gb[s_], hb[s_]
            for m in range(8):
                py = yR.next()

                def mm1(e, py=py, m=m):
                    for h in range(NH):
                        ins = e.matmul(py[:], lhsT=wbm[:, h, m * 128:(m + 1) * 128], rhs=om_[:, h, :], start=(h == 0), stop=(h == NH - 1))
                    return ins
                P.op("pe", mm1, r=[wbm, om_], w=[py])
                t_ = mt.next()
                P.op("dve", lambda e, py=py, t_=t_, m=m: e.tensor_tensor(out=t_[:], in0=py[:], in1=gb_[:, m, :], op=ALU.mult), r=[py, gb_], w=[t_])
                P.op("dve", lambda e, t_=t_, m=m: e.tensor_tensor(out=mg[:, m, :], in0=t_[:], in1=mh_[:, m, :], op=ALU.add), r=[t_, mh_], w=[mgR[m]])
                P.op("dve", lambda e, m=m: e.tensor_scalar(out=h_[:, m, :], in0=h_[:, m, :], scalar1=ALPHA, scalar2=None, op0=ALU.mult),
                     r=[hR[s_][m]], w=[hR[s_][m]])
            for m in range(8):
                py = yR.next()

                def mm2(e, py=py, m=m):
                    for k in range(8):
                        ins = e.matmul(py[:], lhsT=wo[:, k, m * 128:(m + 1) * 128], rhs=mg[:, k, :], start=(k == 0), stop=(k == 7))
                    return ins
                P.op("pe", mm2, r=[wo] + mgR, w=[py])
                P.op("dve", lambda e, py=py, m=m: e.scalar_tensor_tensor(out=h_[:, m, :], in0=py[:], scalar=mc[:, l, 5, m:m + 1], in1=h_[:, m, :],
                                                                         op0=ALU.mult, op1=ALU.add), r=[py, hR[s_][m]], w=[hR[s_][m]])
            rch = [((lambda m=m: h_[:, m, :]), hR[s_][m]) for m in range(8)]
            layer_norm_tile(P, G, l, 1, rch, ps_s, ps_q, sqb, st)
            P.dma("sp", lambda e: e.dma_start(out=hout[:, t0:t0 + NT].rearrange("(k p) t -> p k t", p=128), in_=h_[:]), r=hR[s_])

        load(0)
        for ti in range(ntile):
            do_tile(ti)


_PROG_CACHE = {}


def kernel(**inputs):
    B, S, _ = inputs["x"].shape
    depth = inputs["ada_w"].shape[0]
    key = (S, depth)
    if key not in _PROG_CACHE:
        _PROG_CACHE[key] = build_program(S, depth)
    nc, G = _PROG_CACHE[key]
    shared = prep_shared(inputs, depth)
    n_cores = 8
    in_maps = []
    for c in range(n_cores):
        m = dict(shared)
        m.update(prep_core(inputs, c % B))
        in_maps.append(m)
    res = run_bass_kernel_spmd(nc, in_maps, core_ids=list(range(n_cores)))
    out = np.stack([np.ascontiguousarray(res.results[b]["outT"].T) for b in range(B)], axis=0)
    return out.astype(np.float32)
```

```python
import contextlib
import math
import numpy as np
import ml_dtypes
import concourse.bass as bass
import concourse.mybir as mybir
from concourse.bass_utils import run_bass_kernel_spmd

F32 = mybir.dt.float32
BF16 = mybir.dt.bfloat16
I32 = mybir.dt.int32
AF = mybir.ActivationFunctionType
ALU = mybir.AluOpType
AX = mybir.AxisListType

D = 1024
FF = 2816
NJ = FF // 128
DIN = 6848
NH = 8
DEPTH_FULL = 4
ALPHA = (2.0 * DEPTH_FULL) ** 0.25
LN_EPS = 1e-5
RMS_EPS = 1e-6
ATT_SCALE = 192.0 ** -0.5
NT = 512
ENGS = ("pe", "act", "dve", "pool", "sp")


class Res:
    __slots__ = ("writer", "readers")

    def __init__(self):
        self.writer = None
        self.readers = []


class Buf:
    __slots__ = ("ap", "res")

    def __init__(self, ap):
        self.ap = ap
        self.res = Res()

    def __getitem__(self, k):
        return self.ap[k]


class Op:
    __slots__ = ("eng", "fn", "deps", "is_dma", "has_dep", "sem", "val", "dslot", "prev_on_sem", "phase")

    def __init__(self, eng, fn, is_dma, phase):
        self.eng = eng
        self.fn = fn
        self.deps = []
        self.is_dma = is_dma
        self.has_dep = is_dma
        self.sem = None
        self.val = None
        self.prev_on_sem = None
        self.phase = phase


class Ctx:
    def __init__(self, nc, es, n_dma_sems=32):
        self.nc = nc
        self.esem = {e: es.enter_context(nc.semaphore("s_" + e)) for e in ENGS}
        self.dsems = [es.enter_context(nc.semaphore("d%d" % i)) for i in range(n_dma_sems)]
        self.tick = {e: 0 for e in ENGS}
        self.dcount = [0] * n_dma_sems
        self.dlast = [None] * n_dma_sems
        self.dma_rr = 0
        self.sw_rr = 0
        self.n_sw = 8
        self.n_hw = n_dma_sems - self.n_sw
        self.phase = 0
        self.n_inst = 0


class Sched:
    def __init__(self, ctx):
        self.ctx = ctx
        self.ops = []
        ctx.phase += 1
        self.phase = ctx.phase

    def _add(self, op, reads, writes):
        deps = {}
        for r in reads:
            r = r.res if isinstance(r, Buf) else r
            if r.writer is not None:
                deps[id(r.writer)] = r.writer
        for r in writes:
            r = r.res if isinstance(r, Buf) else r
            if r.writer is not None:
                deps[id(r.writer)] = r.writer
            for o in r.readers:
                deps[id(o)] = o
        for d in deps.values():
            if d is op or d.phase != self.phase:
                continue
            if d.eng == op.eng and op.eng == "pe" and not d.is_dma and not op.is_dma:
                continue
            op.deps.append(d)
            d.has_dep = True
        for r in reads:
            r = r.res if isinstance(r, Buf) else r
            r.readers.append(op)
        for r in writes:
            r = r.res if isinstance(r, Buf) else r
            r.writer = op
            r.readers = []
        self.ops.append(op)
        return op

    def op(self, eng, fn, r=(), w=()):
        return self._add(Op(eng, fn, False, self.phase), r, w)

    def dma(self, eng, fn, r=(), w=()):
        op = Op(eng, fn, True, self.phase)
        ctx = self.ctx
        if eng == "pool":
            op.dslot = ctx.n_hw + ctx.sw_rr % ctx.n_sw
            ctx.sw_rr += 1
        else:
            op.dslot = ctx.dma_rr % ctx.n_hw
            ctx.dma_rr += 1
        return self._add(op, r, w)

    def emit(self):
        ctx = self.ctx
        nc = ctx.nc
        for op in self.ops:
            if op.is_dma:
                k = op.dslot
                prev = ctx.dlast[k]
                op.prev_on_sem = prev if (prev is not None and prev.phase == self.phase) else None
                ctx.dcount[k] += 16
                op.sem = ctx.dsems[k]
                op.val = ctx.dcount[k]
                ctx.dlast[k] = op
            elif op.has_dep:
                ctx.tick[op.eng] += 1
                op.sem = ctx.esem[op.eng]
                op.val = ctx.tick[op.eng]
        per_eng = {e: [] for e in ENGS}
        for op in self.ops:
            per_eng[op.eng].append(op)
        dma_ops = [op for op in self.ops if op.is_dma]
        ctx.n_inst += len(self.ops)

        def run(eng_name, eng):
            known = {}
            for op in per_eng[eng_name]:
                waits = {}
                deps = list(op.deps)
                if op.prev_on_sem is not None:
                    deps.append(op.prev_on_sem)
                for d in deps:
                    key = id(d.sem)
                    if known.get(key, 0) >= d.val:
                        continue
                    if key not in waits or waits[key][1] < d.val:
                        waits[key] = (d.sem, d.val)
                for key, (sem, val) in waits.items():
                    eng.wait_ge(sem, val)
                    known[key] = val
                ins = op.fn(eng)
                if op.is_dma:
                    ins.then_inc(op.sem, 16)
                elif op.has_dep:
                    ins.then_inc(op.sem, 1)
            if eng_name == "sp":
                last = {}
                for d in dma_ops:
                    last[id(d.sem)] = (d.sem, d.val)
                for key, (sem, val) in last.items():
                    if known.get(key, 0) < val:
                        eng.wait_ge(sem, val)

        with nc.Block() as block:
            @block.tensor
            def _(eng):
                run("pe", eng)

            @block.scalar
            def _(eng):
                run("act", eng)

            @block.vector
            def _(eng):
                run("dve", eng)

            @block.gpsimd
            def _(eng):
                run("pool", eng)

            @block.sync
            def _(eng):
                run("sp", eng)


class Phase:
    def __init__(self, G, name):
        self.G = G
        self.name = name
        self.n = 0

    def __enter__(self):
        self.es = contextlib.ExitStack()
        self.S = Sched(self.G.ctx)
        return self

    def __exit__(self, et, ev, tb):
        if et is None:
            self.S.emit()
        self.es.close()
        return False

    def sb(self, shape, dt, name=None):
        self.n += 1
        t = self.es.enter_context(self.G.nc.sbuf_tensor("%s_%s%d" % (self.name, name or "t", self.n), list(shape), dt))
        return Buf(t)

    def op(self, eng, fn, r=(), w=()):
        return self.S.op(eng, fn, r, w)

    def dma(self, eng, fn, r=(), w=()):
        return self.S.dma(eng, fn, r, w)


class Glob:
    pass


def build_program(T, depth, debug_outs=(), stop_after=None):
    nc = bass.Bass("TRN2", target_bir_lowering=False)
    G = Glob()
    G.nc, G.T, G.depth = nc, T, depth
    G.debug_outs = debug_outs

    def din(name, shape, dt=F32):
        return nc.dram_tensor(name, list(shape), dt, kind="ExternalInput").ap()

    def dscr(name, shape, dt):
        kind = "ExternalOutput" if name in debug_outs else "Internal"
        return nc.dram_tensor(name, list(shape), dt, kind=kind).ap()

    I = {}
    I["xT"] = din("xT", [D, T])
    I["ccol"] = din("ccol", [128, 8])
    I["pos"] = din("pos", [1, T], I32)
    I["ada_w"] = din("ada_w", [depth, D, 9 * D])
    I["ada_bc"] = din("ada_bc", [depth, 128, 72])
    I["lngc"] = din("lngc", [depth, 128, 24])
    I["lnbc"] = din("lnbc", [depth, 128, 24])
    for f in (1, 2):
        I["ffn%d_gate" % f] = din("ffn%d_gate" % f, [depth, D, FF])
        I["ffn%d_up" % f] = din("ffn%d_up" % f, [depth, D, FF])
        I["ffn%d_down" % f] = din("ffn%d_down" % f, [depth, FF, D])
    I["w_in"] = din("w_in", [depth, D, DIN])
    I["lbc"] = din("lbc", [depth, 128, 8])
    I["hgnc"] = din("hgnc", [depth, 128, 8])
    I["qnc"] = din("qnc", [depth, 128, 3])
    I["w_uq"] = din("w_uq", [depth, 384, 1536])
    I["kvnc"] = din("kvnc", [depth, 128, 2])
    I["w_ukv"] = din("w_ukv", [depth, 256, 2048])
    I["w_bhg"] = din("w_bhg", [depth, D, D])
    I["w_bmla"] = din("w_bmla", [depth, D, D])
    I["w_out"] = din("w_out", [depth, D, D])
    I["c_ident"] = din("c_ident", [128, 128], BF16)
    I["c_tri"] = din("c_tri", [128, 128], BF16)
    I["c_tri8"] = din("c_tri8", [64, NT], BF16)
    I["c_reset"] = din("c_reset", [128, NT])
    I["c_inv2"] = din("c_inv2", [64, 1])
    G.I = I
    G.outT = nc.dram_tensor("outT", [D, T], F32, kind="ExternalOutput").ap()

    G.WguS = [[dscr("WguS_%d_%d" % (l, f), [NJ, 128, 2, 8, 128], BF16) for f in range(2)] for l in range(depth)]
    G.WdS = [[dscr("WdS_%d_%d" % (l, f), [8, 128, NJ, 128], BF16) for f in range(2)] for l in range(depth)]
    G.WinS = [dscr("WinS_%d" % l, [47, 128, 8, 128], BF16) for l in range(depth)]
    G.WziS = [dscr("WziS_%d" % l, [D, D], BF16) for l in range(depth)]
    G.WuqS = [dscr("WuqS_%d" % l, [24, 128, 3, 128], BF16) for l in range(depth)]
    G.WuknS = [dscr("WuknS_%d" % l, [8, 128, 2, 128], BF16) for l in range(depth)]
    G.WukvV = [dscr("WukvV_%d" % l, [256, D], BF16) for l in range(depth)]
    G.WbhgS = [dscr("WbhgS_%d" % l, [D, D], BF16) for l in range(depth)]
    G.WbmlaS = [dscr("WbmlaS_%d" % l, [D, D], BF16) for l in range(depth)]
    G.WoutS = [dscr("WoutS_%d" % l, [D, D], BF16) for l in range(depth)]
    G.hA = dscr("hA", [D, T], F32)
    G.hB = dscr("hB", [D, T], F32)
    G.cosS = dscr("cosS", [64, T], F32)
    G.sinS = dscr("sinS", [64, T], F32)
    G.qtT = dscr("qtT", [NH, 128, T], BF16)
    G.ktT = dscr("ktT", [NH, 128, T], BF16)
    G.abS = dscr("abS", [T // NT, 128, NH, 3, 8], F32)
    G.vhg = dscr("vhg", [T, D], BF16)
    G.zgT = dscr("zgT", [D, T], BF16)
    G.gaT = dscr("gaT", [D, T], BF16)
    G.gbT = dscr("gbT", [D, T], BF16)
    G.qnT = dscr("qnT", [NH, 128, T], BF16)
    G.qrT = dscr("qrT", [NH, 64, T], BF16)
    G.knT = dscr("knT", [NH, 128, T], BF16)
    G.krT = dscr("krT", [64, T], BF16)
    G.vm = dscr("vm", [NH, 128, T // 128, 128], BF16)
    G.mhT = dscr("mhT", [D, T], F32)
    G.omT = dscr("omT", [NH, 128, T], BF16)

    with contextlib.ExitStack() as es:
        G.ctx = Ctx(nc, es)

        def gsb(name, shape, dt):
            return Buf(es.enter_context(nc.sbuf_tensor(name, list(shape), dt)))

        G.modc = gsb("modc", [128, depth, 9, 8], F32)
        G.omlc = gsb("omlc", [128, depth, 8], F32)
        G.lng = gsb("lng", [128, depth, 24], F32)
        G.lnb = gsb("lnb", [128, depth, 24], F32)
        G.hgn = gsb("hgn", [128, depth, 8], F32)
        G.qn = gsb("qn", [128, depth, 3], F32)
        G.kvn = gsb("kvn", [128, depth, 2], F32)
        G.ident = gsb("ident", [128, 128], BF16)
        G.tri = gsb("tri", [128, 128], BF16)
        G.tri8 = gsb("tri8", [64, NT], BF16)
        G.reset = gsb("reset", [128, NT], F32)
        G.ones_f = gsb("ones_f", [128, 128], F32)
        G.ones_b = gsb("ones_b", [128, 128], BF16)
        G.ps = [Buf(es.enter_context(nc.psum_tensor("ps%d" % i, [128, NT], F32))) for i in range(8)]

        phases = []
        phases.append(("w0", lambda: phase_w0(G)))
        phases.append(("c0", lambda: phase_c0(G)))
        for l in range(depth):
            last = l == depth - 1
            phases.append(("f1_%d" % l, lambda l=l: phase_ffn(G, l, 1, I["xT"] if l == 0 else G.hA, G.hA)))
            phases.append(("m1a_%d" % l, lambda l=l: phase_m1a(G, l, G.hA)))
            phases.append(("m1b_%d" % l, lambda l=l: phase_m1b(G, l, G.hA)))
            phases.append(("hg_%d" % l, lambda l=l: phase_hg(G, l)))
            phases.append(("a1_%d" % l, lambda l=l: phase_a1(G, l)))
            phases.append(("a2_%d" % l, lambda l=l: phase_a2(G, l, G.hA, G.hA)))
            phases.append(("f2_%d" % l, lambda l=l, last=last: phase_ffn(G, l, 2, G.hA, G.outT if last else G.hA)))
        for name, fn in phases:
            fn()
            if stop_after is not None and name == stop_after:
                break
    G.n_inst = G.ctx.n_inst
    return nc, G


WIN_CHUNKS = ([(128 * i, 128) for i in range(8)] + [(1024 + 128 * i, 128) for i in range(8)]
              + [(3072 + 128 * i, 128) for i in range(8)] + [(4096 + 128 * i, 128) for i in range(3)]
              + [(4480 + 128 * i, 128) for i in range(2)] + [(4800 + 128 * i, 128) for i in range(8)]
              + [(5824 + 128 * i, 128) for i in range(8)])
C_ZQ, C_ZF, C_ZG, C_CQ, C_CKV, C_GA, C_GB, C_KRA, C_KRB = 0, 8, 16, 24, 27, 29, 37, 45, 46


def cast_layer(P, G, l):
    I = G.I
    if True:
        def cp(dst, src):
            P.dma("pool", lambda e, d=dst, s=src: e.dma_start(out=d, in_=s))

        if True:
            for f in range(2):
                gate, up, down = I["ffn%d_gate" % (f + 1)][l], I["ffn%d_up" % (f + 1)][l], I["ffn%d_down" % (f + 1)][l]
                for j in range(NJ):
                    for g, W in enumerate((gate, up)):
                        cp(G.WguS[l][f][j, :, g, :, :], W[:, j * 128:(j + 1) * 128].rearrange("(k p) m -> p k m", p=128))
                for m in range(8):
                    cp(G.WdS[l][f][m], down[:, m * 128:(m + 1) * 128].rearrange("(j p) m -> p j m", p=128))
            win = I["w_in"][l]
            for c, (c0, wd) in enumerate(WIN_CHUNKS):
                cp(G.WinS[l][c], win[:, c0:c0 + wd].rearrange("(k p) m -> p k m", p=128))
            for o in (0, 64):
                cp(G.WinS[l][C_KRA, :, :, o:o + 64], win[:, 4736:4800].rearrange("(k p) m -> p k m", p=128))
                cp(G.WinS[l][C_KRB, :, :, o:o + 32], win[:, 4768:4800].rearrange("(k p) m -> p k m", p=128))
                cp(G.WinS[l][C_KRB, :, :, o + 32:o + 64], win[:, 4736:4768].rearrange("(k p) m -> p k m", p=128))
            cp(G.WziS[l], win[:, 2048:3072])
            uq = I["w_uq"][l]
            for h in range(NH):
                b = 192 * h
                cp(G.WuqS[l][3 * h], uq[:, b:b + 128].rearrange("(k p) m -> p k m", p=128))
                for o in (0, 64):
                    cp(G.WuqS[l][3 * h + 1, :, :, o:o + 64], uq[:, b + 128:b + 192].rearrange("(k p) m -> p k m", p=128))
                    cp(G.WuqS[l][3 * h + 2, :, :, o:o + 32], uq[:, b + 160:b + 192].rearrange("(k p) m -> p k m", p=128))
                    cp(G.WuqS[l][3 * h + 2, :, :, o + 32:o + 64], uq[:, b + 128:b + 160].rearrange("(k p) m -> p k m", p=128))
            ukv = I["w_ukv"][l]
            for h in range(NH):
                cp(G.WuknS[l][h], ukv[:, 256 * h:256 * h + 128].rearrange("(k p) m -> p k m", p=128))
                cp(G.WukvV[l][:, 128 * h:128 * h + 128], ukv[:, 256 * h + 128:256 * h + 256])
            cp(G.WbhgS[l], I["w_bhg"][l])
            cp(G.WbmlaS[l], I["w_bmla"][l])
            cp(G.WoutS[l], I["w_out"][l])


def phase_w0(G):
    with Phase(G, "w0") as P:
        cast_layer(P, G, 0)


def phase_c0(G):
    I = G.I
    T, depth = G.T, G.depth
    with Phase(G, "c0") as P:
        ld = lambda dst, src, w: P.dma("sp", lambda e: e.dma_start(out=dst, in_=src), w=w)
        ld(G.lng[:], I["lngc"].rearrange("l p c -> p l c"), [G.lng])
        ld(G.lnb[:], I["lnbc"].rearrange("l p c -> p l c"), [G.lnb])
        ld(G.hgn[:], I["hgnc"].rearrange("l p c -> p l c"), [G.hgn])
        ld(G.qn[:], I["qnc"].rearrange("l p c -> p l c"), [G.qn])
        ld(G.kvn[:], I["kvnc"].rearrange("l p c -> p l c"), [G.kvn])
        ld(G.ident[:], I["c_ident"], [G.ident])
        ld(G.tri[:], I["c_tri"], [G.tri])
        ld(G.tri8[:], I["c_tri8"], [G.tri8])
        ld(G.reset[:], I["c_reset"], [G.reset])
        P.op("pool", lambda e: e.memset(G.ones_f[:], 1.0), w=[G.ones_f])
        P.op("pool", lambda e: e.memset(G.ones_b[:], 1.0), w=[G.ones_b])
        cc = P.sb([128, 8], F32)
        cond = P.sb([128, 8], F32)
        adab = P.sb([128, depth, 72], F32)
        ld(cc[:], I["ccol"], [cc])
        ld(adab[:], I["ada_bc"].rearrange("l p c -> p l c"), [adab])
        P.op("act", lambda e: e.activation(out=cond[:], in_=cc[:], func=AF.Silu), r=[cc], w=[cond])
        pan = [P.sb([128, 8, D], F32) for _ in range(2)]
        psA = G.ps[0]
        n = 0
        for l in range(depth):
            for j in range(9):
                pb = pan[n % 2]
                n += 1
                P.dma("sp", lambda e, pb=pb, l=l, j=j: e.dma_start(
                    out=pb[:], in_=I["ada_w"][l][:, j * D:(j + 1) * D].rearrange("(k p) n -> p k n", p=128)), w=[pb])

                def mm(e, pb=pb, l=l, j=j):
                    for m in range(8):
                        col = l * 72 + j * 8 + m
                        for k in range(8):
                            ins = e.matmul(psA[:, col:col + 1], lhsT=pb[:, k, m * 128:(m + 1) * 128], rhs=cond[:, k:k + 1],
                                           start=(k == 0), stop=(k == 7))
                    return ins
                P.op("pe", mm, r=[pb, cond], w=[psA])
        mc = G.modc
        mflat = lambda: mc[:].rearrange("p l j k -> p (l j k)")
        P.op("dve", lambda e: e.tensor_tensor(out=mflat(), in0=psA[:, 0:depth * 72], in1=adab[:].rearrange("p l c -> p (l c)"),
                                              op=ALU.add), r=[psA, adab], w=[mc])
        for j in (1, 4, 5, 7):
            P.op("dve", lambda e, j=j: e.tensor_scalar(out=mc[:, :, j, :], in0=mc[:, :, j, :], scalar1=1.0, scalar2=None, op0=ALU.add),
                 r=[mc], w=[mc])
        for j in (2, 8):
            P.op("dve", lambda e, j=j: e.tensor_scalar(out=mc[:, :, j, :], in0=mc[:, :, j, :], scalar1=0.5, scalar2=0.5,
                                                       op0=ALU.mult, op1=ALU.add), r=[mc], w=[mc])
        lb = P.sb([128, depth, 8], F32)
        ex = P.sb([128, depth, 8], F32)
        mx = P.sb([128, 8], F32)
        sm = P.sb([128, 8], F32)
        cs = P.sb([128, 8], F32)
        ld(lb[:], I["lbc"].rearrange("l p c -> p l c"), [lb])
        P.op("dve", lambda e: e.tensor_copy(out=mx[:], in_=lb[:, 0, :]), r=[lb], w=[mx])
        for l in range(1, depth):
            P.op("dve", lambda e, l=l: e.tensor_tensor(out=mx[:], in0=mx[:], in1=lb[:, l, :], op=ALU.max), r=[lb, mx], w=[mx])
        for l in range(depth):
            P.op("dve", lambda e, l=l: e.tensor_tensor(out=ex[:, l, :], in0=lb[:, l, :], in1=mx[:], op=ALU.subtract), r=[lb, mx], w=[ex])
        P.op("act", lambda e: e.activation(out=ex[:], in_=ex[:], func=AF.Exp), r=[ex], w=[ex])
        P.op("dve", lambda e: e.tensor_copy(out=sm[:], in_=ex[:, 0, :]), r=[ex], w=[sm])
        for l in range(1, depth):
            P.op("dve", lambda e, l=l: e.tensor_tensor(out=sm[:], in0=sm[:], in1=ex[:, l, :], op=ALU.add), r=[ex, sm], w=[sm])
        P.op("dve", lambda e: e.reciprocal(out=sm[:], in_=sm[:]), r=[sm], w=[sm])
        for l in range(depth):
            P.op("dve", lambda e, l=l: e.tensor_tensor(out=ex[:, l, :], in0=ex[:, l, :], in1=sm[:], op=ALU.mult), r=[ex, sm], w=[ex])
        P.op("dve", lambda e: e.tensor_copy(out=cs[:], in_=ex[:, 0, :]), r=[ex], w=[cs])
        om = G.omlc
        for l in range(depth):
            if l > 0:
                P.op("dve", lambda e, l=l: e.tensor_tensor(out=cs[:], in0=cs[:], in1=ex[:, l, :], op=ALU.add), r=[ex, cs], w=[cs])
            P.op("dve", lambda e, l=l: e.tensor_tensor(out=om[:, l, :], in0=cs[:], in1=ex[:, 0, :], op=ALU.subtract), r=[ex, cs], w=[om])
        P.op("dve", lambda e: e.tensor_scalar(out=om[:], in0=om[:], scalar1=-1.0, scalar2=1.0, op0=ALU.mult, op1=ALU.add), r=[om], w=[om])
        inv = P.sb([64, 1], F32)
        ld(inv[:], I["c_inv2"], [inv])
        TC = min(T, 2048)
        pi_ = P.sb([64, TC], I32)
        ang = P.sb([64, TC], F32)
        angc = P.sb([64, TC], F32)
        tmp = P.sb([64, TC], F32)
        ki = P.sb([64, TC], I32)
        kf = P.sb([64, TC], F32)
        tab = P.sb([64, TC], F32)
        tab2 = P.sb([64, TC], F32)
        C1 = 6.28125
        C2 = 2 * math.pi - C1
        PI = math.pi

        def reduce_angle(src, half):
            P.op("dve", lambda e: e.tensor_scalar(out=tmp[:], in0=src[:], scalar1=1.0 / (2 * PI), scalar2=half, op0=ALU.mult, op1=ALU.add),
                 r=[src], w=[tmp])
            P.op("dve", lambda e: e.tensor_copy(out=ki[:], in_=tmp[:]), r=[tmp], w=[ki])
            P.op("dve", lambda e: e.tensor_copy(out=kf[:], in_=ki[:]), r=[ki], w=[kf])
            P.op("dve", lambda e: e.scalar_tensor_tensor(out=tmp[:], in0=kf[:], scalar=-C1, in1=src[:], op0=ALU.mult, op1=ALU.add),
                 r=[kf, src], w=[tmp])
            P.op("dve", lambda e: e.scalar_tensor_tensor(out=tmp[:], in0=kf[:], scalar=-C2, in1=tmp[:], op0=ALU.mult, op1=ALU.add),
                 r=[kf, tmp], w=[tmp])
            P.op("dve", lambda e: e.tensor_scalar(out=kf[:], in0=tmp[:], scalar1=PI, scalar2=-2 * PI, op0=ALU.is_gt, op1=ALU.mult),
                 r=[tmp], w=[kf])
            P.op("dve", lambda e: e.tensor_tensor(out=tmp[:], in0=tmp[:], in1=kf[:], op=ALU.add), r=[tmp, kf], w=[tmp])
            P.op("dve", lambda e: e.tensor_scalar(out=kf[:], in0=tmp[:], scalar1=-PI, scalar2=2 * PI, op0=ALU.is_lt, op1=ALU.mult),
                 r=[tmp], w=[kf])
            P.op("dve", lambda e: e.tensor_tensor(out=tmp[:], in0=tmp[:], in1=kf[:], op=ALU.add), r=[tmp, kf], w=[tmp])
            P.op("dve", lambda e: e.tensor_scalar(out=tmp[:], in0=tmp[:], scalar1=-PI, scalar2=PI, op0=ALU.max, op1=ALU.min),
                 r=[tmp], w=[tmp])

        for t0 in range(0, T, TC):
            P.dma("sp", lambda e, t0=t0: e.dma_start(out=pi_[:], in_=I["pos"][:, t0:t0 + TC].partition_broadcast(64)), w=[pi_])
            P.op("dve", lambda e: e.tensor_copy(out=ang[:], in_=pi_[:]), r=[pi_], w=[ang])
            P.op("dve", lambda e: e.tensor_scalar(out=ang[:], in0=ang[:], scalar1=inv[:, 0:1], scalar2=None, op0=ALU.mult),
                 r=[ang, inv], w=[ang])
            reduce_angle(ang, 0.5)
            P.op("act", lambda e: e.activation(out=tab2[0:32, :], in_=tmp[0:32, :], func=AF.Sin, scale=-1.0), r=[tmp], w=[tab2])
            P.op("act", lambda e: e.activation(out=tab2[32:64, :], in_=tmp[32:64, :], func=AF.Sin), r=[tmp], w=[tab2])
            P.dma("sp", lambda e, t0=t0: e.dma_start(out=G.sinS[:, t0:t0 + TC], in_=tab2[:]), r=[tab2])
            P.op("dve", lambda e: e.tensor_scalar(out=angc[:], in0=tmp[:], scalar1=0.5 * PI, scalar2=None, op0=ALU.add), r=[tmp], w=[angc])
            P.op("dve", lambda e: e.tensor_scalar(out=kf[:], in0=angc[:], scalar1=PI, scalar2=-2 * PI, op0=ALU.is_gt, op1=ALU.mult),
                 r=[angc], w=[kf])
            P.op("dve", lambda e: e.tensor_tensor(out=angc[:], in0=angc[:], in1=kf[:], op=ALU.add), r=[angc, kf], w=[angc])
            P.op("dve", lambda e: e.tensor_scalar(out=angc[:], in0=angc[:], scalar1=-PI, scalar2=PI, op0=ALU.max, op1=ALU.min),
                 r=[angc], w=[angc])
            P.op("act", lambda e: e.activation(out=tab[:], in_=angc[:], func=AF.Sin), r=[angc], w=[tab])
            P.dma("sp", lambda e, t0=t0: e.dma_start(out=G.cosS[:, t0:t0 + TC], in_=tab[:]), r=[tab])

def layer_norm_tile(P, G, l, lnidx, rch, ps_s, ps_q, sqb, st):
    ones = G.ones_f
    for m in range(8):
        ap, rs = rch[m]
        sb_ = sqb[m % 2]
        P.op("act", lambda e, ap=ap, sb_=sb_: e.activation(out=sb_[:], in_=ap(), func=AF.Square), r=[rs], w=[sb_])
        P.op("pe", lambda e, ap=ap, m=m: e.matmul(ps_s[:], lhsT=ones[:], rhs=ap(), start=(m == 0), stop=(m == 7)), r=[rs, ones], w=[ps_s])
        P.op("pe", lambda e, sb_=sb_, m=m: e.matmul(ps_q[:], lhsT=ones[:], rhs=sb_[:], start=(m == 0), stop=(m == 7)), r=[sb_, ones], w=[ps_q])
    mean, msq, rstd = st["mean"], st["msq"], st["rstd"]
    P.op("dve", lambda e: e.tensor_scalar(out=mean[:], in0=ps_s[:], scalar1=1.0 / D, scalar2=None, op0=ALU.mult), r=[ps_s], w=[mean])
    P.op("dve", lambda e: e.tensor_tensor(out=msq[:], in0=mean[:], in1=mean[:], op=ALU.mult), r=[mean], w=[msq])
    P.op("dve", lambda e: e.scalar_tensor_tensor(out=msq[:], in0=ps_q[:], scalar=1.0 / D, in1=msq[:], op0=ALU.mult, op1=ALU.subtract),
         r=[ps_q, msq], w=[msq])
    P.op("act", lambda e: e.activation(out=rstd[:], in_=msq[:], func=AF.Sqrt, bias=LN_EPS, scale=1.0), r=[msq], w=[rstd])
    P.op("dve", lambda e: e.reciprocal(out=rstd[:], in_=rstd[:]), r=[rstd], w=[rstd])
    for m in range(8):
        ap, rs = rch[m]
        P.op("dve", lambda e, ap=ap: e.tensor_tensor(out=ap(), in0=ap(), in1=mean[:], op=ALU.subtract), r=[rs, mean], w=[rs])
        P.op("dve", lambda e, ap=ap: e.tensor_tensor(out=ap(), in0=ap(), in1=rstd[:], op=ALU.mult), r=[rs, rstd], w=[rs])
        c = lnidx * 8 + m
        P.op("act", lambda e, ap=ap, c=c: e.activation(out=ap(), in_=ap(), func=AF.Identity, bias=G.lnb[:, l, c:c + 1],
                                                        scale=G.lng[:, l, c:c + 1]), r=[rs], w=[rs])


def phase_ffn(G, l, f, hin, hout):
    T = G.T
    NS = min(2, T // NT)
    ST = NS * NT
    nst = T // ST
    ib, isc, ig = (0, 1, 2) if f == 1 else (6, 7, 8)
    lnidx = 0 if f == 1 else 2
    mc = G.modc
    with Phase(G, "f%d_%d" % (f, l)) as P:
        wgu = [P.sb([128, 2, 8, 128], BF16) for _ in range(3)]
        wdm = [P.sb([128, NJ, 128], BF16) for _ in range(3)]
        hb = [P.sb([128, 8, ST], F32) for _ in range(2)]
        hR = [[[Res() for _ in range(NS)] for _ in range(8)] for _ in range(2)]
        ub = [P.sb([128, 8, ST], BF16) for _ in range(2)]
        hid = P.sb([128, NJ, ST], BF16)
        hidR = [[Res() for _ in range(NS)] for _ in range(NJ)]
        sg = [P.sb([128, NT], F32) for _ in range(2)]
        sqb = [P.sb([128, NT], F32) for _ in range(2)]
        st = {k: P.sb([128, NT], F32) for k in ("mean", "msq", "rstd")}
        psG, psU, psY, ps_s, ps_q = G.ps[0:2], G.ps[2:4], G.ps[4:6], G.ps[6], G.ps[7]
        allR = lambda slot: [hR[slot][m][s] for m in range(8) for s in range(NS)]

        def load_h(sti):
            slot = sti % 2
            P.dma("sp", lambda e: e.dma_start(out=hb[slot][:], in_=hin[:, sti * ST:(sti + 1) * ST].rearrange("(k p) t -> p k t", p=128)),
                  w=allR(slot))

        nwl = [0]

        def load_wgu(j):
            b = wgu[nwl[0] % 3]
            nwl[0] += 1
            P.dma("sp", lambda e: e.dma_start(out=b[:], in_=G.WguS[l][f - 1][j]), w=[b])
            return b

        nwd = [0]

        def load_wd(m):
            b = wdm[nwd[0] % 3]
            nwd[0] += 1
            P.dma("sp", lambda e: e.dma_start(out=b[:], in_=G.WdS[l][f - 1][m]), w=[b])
            return b

        load_h(0)
        cnt = [0]

        def part_u(sti):
            slot = sti % 2
            h, u = hb[slot], ub[slot]
            for m in range(8):
                for s in range(NS):
                    sl = slice(s * NT, (s + 1) * NT)
                    P.op("act", lambda e, m=m, sl=sl: e.activation(out=u[:, m, sl], in_=h[:, m, sl], func=AF.Identity,
                                                                 bias=mc[:, l, ib, m:m + 1], scale=mc[:, l, isc, m:m + 1]),
                         r=[hR[slot][m][s]], w=[u])
            for m in range(8):
                for s in range(NS):
                    sl = slice(s * NT, (s + 1) * NT)
                    P.op("dve", lambda e, m=m, sl=sl: e.tensor_scalar(out=h[:, m, sl], in0=h[:, m, sl], scalar1=ALPHA, scalar2=None,
                                                                     op0=ALU.mult), r=[hR[slot][m][s]], w=[hR[slot][m][s]])

        def part_ln(sti):
            slot = sti % 2
            h = hb[slot]
            for s in range(NS):
                sl = slice(s * NT, (s + 1) * NT)
                rch = [((lambda m=m, sl=sl: h[:, m, sl]), hR[slot][m][s]) for m in range(8)]
                layer_norm_tile(P, G, l, lnidx, rch, ps_s, ps_q, sqb, st)
            P.dma("sp", lambda e: e.dma_start(out=hout[:, sti * ST:(sti + 1) * ST].rearrange("(k p) t -> p k t", p=128), in_=h[:]),
                  r=allR(slot))

        def part_main(sti, deferred):
            slot = sti % 2
            h, u = hb[slot], ub[slot]
            wq = [load_wgu(0), load_wgu(1)]
            for j in range(NJ):
                w = wq.pop(0)
                if j + 2 < NJ:
                    wq.append(load_wgu(j + 2))
                for s in range(NS):
                    sl = slice(s * NT, (s + 1) * NT)
                    pg, pu, sgb = psG[cnt[0] % 2], psU[cnt[0] % 2], sg[cnt[0] % 2]
                    cnt[0] += 1
                    for g, pt in ((0, pg), (1, pu)):
                        def mm(e, g=g, pt=pt, w=w, sl=sl):
                            for k in range(8):
                                ins = e.matmul(pt[:], lhsT=w[:, g, k, :], rhs=u[:, k, sl], start=(k == 0), stop=(k == 7))
                            return ins
                        P.op("pe", mm, r=[w, u], w=[pt])
                    P.op("act", lambda e, pg=pg, sgb=sgb: e.activation(out=sgb[:], in_=pg[:], func=AF.Silu), r=[pg], w=[sgb])
                    P.op("dve", lambda e, pu=pu, sgb=sgb, j=j, sl=sl: e.tensor_tensor(out=hid[:, j, sl], in0=pu[:], in1=sgb[:], op=ALU.mult),
                         r=[pu, sgb], w=[hidR[j][s]])
                if j == 2:
                    if deferred is not None:
                        deferred()
                    if sti + 1 < nst:
                        load_h(sti + 1)
            if sti + 1 < nst:
                part_u(sti + 1)
            wdq = [load_wd(0), load_wd(1)]
            for m in range(8):
                w = wdq.pop(0)
                if m + 2 < 8:
                    wdq.append(load_wd(m + 2))
                for s in range(NS):
                    sl = slice(s * NT, (s + 1) * NT)
                    py = psY[cnt[0] % 2]
                    cnt[0] += 1

                    def mmd(e, py=py, w=w, sl=sl):
                        for j in range(NJ):
                            ins = e.matmul(py[:], lhsT=w[:, j, :], rhs=hid[:, j, sl], start=(j == 0), stop=(j == NJ - 1))
                        return ins
                    P.op("pe", mmd, r=[w] + [hidR[j][s] for j in range(NJ)], w=[py])
                    P.op("dve", lambda e, py=py, m=m, sl=sl: e.scalar_tensor_tensor(
                        out=h[:, m, sl], in0=py[:], scalar=mc[:, l, ig, m:m + 1], in1=h[:, m, sl], op0=ALU.mult, op1=ALU.add),
                        r=[py, hR[slot][m][s]], w=[hR[slot][m][s]])

        part_u(0)
        for sti in range(nst):
            part_main(sti, (lambda p=sti - 1: part_ln(p)) if sti > 0 else None)
        part_ln(nst - 1)


def _col(v):
    v = np.asarray(v)
    n = v.shape[-1] // 128
    return np.ascontiguousarray(np.swapaxes(v.reshape(v.shape[:-1] + (n, 128)), -1, -2))


def host_consts():
    ident = np.eye(128, dtype=np.float32).astype(ml_dtypes.bfloat16)
    tri = (np.arange(128)[:, None] <= np.arange(128)[None, :]).astype(np.float32).astype(ml_dtypes.bfloat16)
    reset = np.ones((128, NT), np.float32)
    reset[:, ::64] = 0.0
    inv = (1.0 / (np.float32(10000.0) ** (np.arange(0, 64, 2, dtype=np.float32) / np.float32(64)))).astype(np.float32)
    inv2 = np.concatenate([inv, inv]).reshape(64, 1).astype(np.float32)
    tri8 = np.ascontiguousarray(np.tile(tri[0:64, 0:64], (1, 8)))
    return {"c_ident": ident, "c_tri": tri, "c_tri8": tri8, "c_reset": reset, "c_inv2": inv2}


def prep_shared(inp, depth):
    f = lambda k: np.ascontiguousarray(np.asarray(inp[k], dtype=np.float32)[:depth])
    sh = {}
    for k in ("ada_w", "ffn1_gate", "ffn1_up", "ffn1_down", "ffn2_gate", "ffn2_up", "ffn2_down", "w_in"):
        sh[k] = f(k)
    sh["w_uq"] = f("mla_w_uq")
    sh["w_ukv"] = f("mla_w_ukv")
    sh["w_bhg"] = f("w_branch_hg")
    sh["w_bmla"] = f("w_branch_mla")
    sh["w_out"] = f("w_out")
    sh["ada_bc"] = _col(f("ada_b"))
    sh["lngc"] = _col(f("ln_g").reshape(depth, 3 * D))
    sh["lnbc"] = _col(f("ln_b").reshape(depth, 3 * D))
    sh["lbc"] = _col(f("hg_lower_bound"))
    sh["hgnc"] = _col(f("hg_norm_g").reshape(depth, D))
    sh["qnc"] = _col(f("mla_q_norm_g"))
    sh["kvnc"] = _col(f("mla_kv_norm_g"))
    sh.update(host_consts())
    return sh


def prep_core(inp, b):
    x = np.asarray(inp["x"], dtype=np.float32)[b]
    return {
        "xT": np.ascontiguousarray(x.T),
        "ccol": _col(np.asarray(inp["c"], dtype=np.float32)[b]),
        "pos": np.ascontiguousarray(np.asarray(inp["positions"]).astype(np.int32)[b].reshape(1, -1)),
    }


class Rot:
    def __init__(self, bufs):
        self.bufs = bufs
        self.i = 0

    def next(self):
        b = self.bufs[self.i % len(self.bufs)]
        self.i += 1
        return b


def _load_u(P, G, l, hin, hb, ub, ti, isc, ib):
    slot = ti % 2
    P.dma("sp", lambda e: e.dma_start(out=hb[slot][:], in_=hin[:, ti * NT:(ti + 1) * NT].rearrange("(k p) t -> p k t", p=128)),
          w=[hb[slot]])


def _make_u(P, G, l, hb, ub, ti, isc, ib):
    slot = ti % 2
    h, u = hb[slot], ub[slot]
    mc = G.modc
    for m in range(8):
        P.op("act", lambda e, m=m: e.activation(out=u[:, m, :], in_=h[:, m, :], func=AF.Identity, bias=mc[:, l, ib, m:m + 1],
                                                scale=mc[:, l, isc, m:m + 1]), r=[h], w=[u])
    return u


def phase_m1a(G, l, hin):
    T = G.T
    ntile = T // NT
    with Phase(G, "m1a_%d" % l) as P:
        wres = P.sb([128, 24, 8, 128], BF16)
        wR = [Res() for _ in range(24)]
        wzi = P.sb([128, 8, D], BF16)
        hb = [P.sb([128, 8, NT], F32) for _ in range(2)]
        ub = [P.sb([128, 8, NT], BF16) for _ in range(2)]
        tmp = [{k: P.sb([128, NT], F32) for k in ("hk", "lf", "G", "D1", "E1", "E2", "hq")} for _ in range(2)]
        abt = [P.sb([128, 3, 8], F32) for _ in range(2)]
        stage = Rot([P.sb([128, NT], BF16) for _ in range(6)])
        vst = Rot([P.sb([128, D], BF16) for _ in range(2)])
        psr = Rot(G.ps)
        _load_u(P, G, l, hin, hb, ub, 0, 4, 3)
        for h in range(8):
            for c in (8 + h, h):
                P.dma("sp", lambda e, c=c: e.dma_start(out=wres[:, c], in_=G.WinS[l][c]), w=[wR[c]])
        for c in range(16, 24):
            P.dma("sp", lambda e, c=c: e.dma_start(out=wres[:, c], in_=G.WinS[l][c]), w=[wR[c]])
        P.dma("sp", lambda e: e.dma_start(out=wzi[:], in_=G.WziS[l].rearrange("(k p) n -> p k n", p=128)), w=[wzi])

        def proj(c, u):
            ps = psr.next()

            def mm(e):
                for k in range(8):
                    ins = e.matmul(ps[:], lhsT=wres[:, c, k, :], rhs=u[:, k, :], start=(k == 0), stop=(k == 7))
                return ins
            P.op("pe", mm, r=[wR[c], u], w=[ps])
            return ps

        def do_tile(ti):
            t0 = ti * NT
            if ti + 1 < ntile:
                _load_u(P, G, l, hin, hb, ub, ti + 1, 4, 3)
            u = _make_u(P, G, l, hb, ub, ti, 4, 3)
            def head_stages(h):
                t = tmp[h % 2]
                hk, lf, Gt, D1, E1, E2, hq = (t[k] for k in ("hk", "lf", "G", "D1", "E1", "E2", "hq"))
                ab = abt[h % 2]
                gv = lambda: Gt[:].rearrange("p (c j) -> p c j", j=64)
                dv = lambda: D1[:].rearrange("p (c j) -> p c j", j=64)
                ev = lambda: E1[:].rearrange("p (c j) -> p c j", j=64)

                def s0():
                    pz = proj(8 + h, u)
                    P.op("act", lambda e: e.activation(out=hk[:], in_=pz[:], func=AF.Sigmoid, scale=-1.0), r=[pz], w=[hk])

                def s1():
                    P.op("dve", lambda e: e.tensor_scalar(out=hk[:], in0=hk[:], scalar1=G.omlc[:, l, h:h + 1], scalar2=None, op0=ALU.mult),
                         r=[hk], w=[hk])
                    P.op("act", lambda e: e.activation(out=lf[:], in_=hk[:], func=AF.Ln, scale=-1.0, bias=1.0), r=[hk], w=[lf])

                def s2():
                    P.op("dve", lambda e: e.tensor_tensor_scan(out=Gt[:], data0=G.reset[:], data1=lf[:], initial=0.0, op0=ALU.mult, op1=ALU.add),
                         r=[lf], w=[Gt])
                    P.op("dve", lambda e: e.tensor_tensor(out=dv(), in0=gv(), in1=gv()[:, :, 31:32].to_broadcast([128, 8, 64]), op=ALU.subtract),
                         r=[Gt], w=[D1])
                    P.op("dve", lambda e: e.tensor_scalar(out=D1[:], in0=D1[:], scalar1=-40.0, scalar2=40.0, op0=ALU.max, op1=ALU.min),
                         r=[D1], w=[D1])

                def s3():
                    P.op("act", lambda e: e.activation(out=E1[:], in_=D1[:], func=AF.Exp), r=[D1], w=[E1])
                    P.op("act", lambda e: e.activation(out=E2[:], in_=D1[:], func=AF.Exp, scale=-1.0), r=[D1], w=[E2])
                    P.op("act", lambda e: e.activation(out=ab[:, 0, :], in_=gv()[:, :, 31], func=AF.Exp), r=[Gt], w=[ab])
                    pq = proj(h, u)
                    P.op("act", lambda e: e.activation(out=hq[:], in_=pq[:], func=AF.Silu), r=[pq], w=[hq])

                def s4():
                    P.op("dve", lambda e: e.tensor_copy(out=ab[:, 1, :], in_=ev()[:, :, 63]), r=[E1], w=[ab])
                    P.op("dve", lambda e: e.tensor_tensor(out=ab[:, 2, :], in0=ab[:, 0, :], in1=ab[:, 1, :], op=ALU.mult), r=[ab], w=[ab])
                    P.dma("sp", lambda e: e.dma_start(out=G.abS[ti][:, h], in_=ab[:]), r=[ab])
                    qs, ks = stage.next(), stage.next()
                    P.op("dve", lambda e: e.tensor_tensor(out=qs[:], in0=hq[:], in1=E1[:], op=ALU.mult), r=[hq, E1], w=[qs])
                    P.dma("sp", lambda e: e.dma_start(out=G.qtT[h][:, t0:t0 + NT], in_=qs[:]), r=[qs])
                    P.op("dve", lambda e: e.tensor_tensor(out=ks[:], in0=hk[:], in1=E2[:], op=ALU.mult), r=[hk, E2], w=[ks])
                    P.dma("sp", lambda e: e.dma_start(out=G.ktT[h][:, t0:t0 + NT], in_=ks[:]), r=[ks])
                return [s0, s1, s2, s3, s4]

            for hp in range(0, 8, 2):
                sa, sb_ = head_stages(hp), head_stages(hp + 1)
                for i in range(5):
                    sa[i]()
                    sb_[i]()
            for j in range(8):
                pg = proj(16 + j, u)
                zs = stage.next()
                P.op("act", lambda e, pg=pg, zs=zs: e.activation(out=zs[:], in_=pg[:], func=AF.Silu), r=[pg], w=[zs])
                P.dma("sp", lambda e, zs=zs, j=j, t0=t0: e.dma_start(out=G.zgT[j * 128:(j + 1) * 128, t0:t0 + NT], in_=zs[:]), r=[zs])
            for tb in range(4):
                vs = vst.next()
                for half in range(2):
                    ps = psr.next()

                    def mmv(e, ps=ps, tb=tb, half=half):
                        for k in range(8):
                            ins = e.matmul(ps[:], lhsT=u[:, k, tb * 128:(tb + 1) * 128], rhs=wzi[:, k, half * 512:(half + 1) * 512],
                                           start=(k == 0), stop=(k == 7))
                        return ins
                    P.op("pe", mmv, r=[wzi, u], w=[ps])
                    if half == 0:
                        P.op("act", lambda e, ps=ps, vs=vs: e.activation(out=vs[:, 0:512], in_=ps[:], func=AF.Copy), r=[ps], w=[vs])
                    else:
                        P.op("dve", lambda e, ps=ps, vs=vs: e.tensor_copy(out=vs[:, 512:1024], in_=ps[:]), r=[ps], w=[vs])
                P.dma("sp", lambda e, vs=vs, tb=tb, t0=t0: e.dma_start(out=G.vhg[t0 + tb * 128:t0 + (tb + 1) * 128, :], in_=vs[:]), r=[vs])
        for ti in range(ntile):
            do_tile(ti)


def phase_m1b(G, l, hin):
    T = G.T
    ntile = T // NT
    with Phase(G, "m1b_%d" % l) as P:
        wres = P.sb([128, 23, 8, 128], BF16)
        wR = [Res() for _ in range(23)]
        wuq = P.sb([128, 24, 3, 128], BF16)
        wukn = P.sb([128, 8, 2, 128], BF16)
        wukv = P.sb([128, 2, D], BF16)
        hb = [P.sb([128, 8, NT], F32) for _ in range(2)]
        ub = [P.sb([128, 8, NT], BF16) for _ in range(2)]
        cs = [P.sb([64, 2, NT], F32) for _ in range(2)]
        cq = P.sb([128, 3, NT], F32)
        cqn = P.sb([128, 3, NT], BF16)
        ckv = P.sb([128, 2, NT], F32)
        ckvn = P.sb([128, 2, NT], BF16)
        sqb = Rot([P.sb([128, NT], F32) for _ in range(3)])
        rstd = Rot([P.sb([128, NT], F32) for _ in range(2)])
        rt = Rot([P.sb([64, NT], F32) for _ in range(4)])
        stage = Rot([P.sb([128, NT], BF16) for _ in range(6)])
        vst = Rot([P.sb([128, D], BF16) for _ in range(2)])
        psr = Rot(G.ps)
        _load_u(P, G, l, hin, hb, ub, 0, 4, 3)
        for i, c in enumerate(list(range(24, 29)) + [45, 46] + list(range(29, 45))):
            dst = {45: 21, 46: 22}.get(c, c - 24)
            P.dma("sp", lambda e, c=c, dst=dst: e.dma_start(out=wres[:, dst], in_=G.WinS[l][c]), w=[wR[dst]])
        P.dma("sp", lambda e: e.dma_start(out=wuq[:], in_=G.WuqS[l].rearrange("s p k m -> p s k m")), w=[wuq])
        P.dma("sp", lambda e: e.dma_start(out=wukn[:], in_=G.WuknS[l].rearrange("s p k m -> p s k m")), w=[wukn])
        P.dma("sp", lambda e: e.dma_start(out=wukv[:], in_=G.WukvV[l].rearrange("(k p) n -> p k n", p=128)), w=[wukv])

        def proj(c, u, mrows=128):
            ps = psr.next()

            def mm(e):
                for k in range(8):
                    ins = e.matmul(ps[0:mrows, :], lhsT=wres[:, c, k, 0:mrows], rhs=u[:, k, :], start=(k == 0), stop=(k == 7))
                return ins
            P.op("pe", mm, r=[wR[c], u], w=[ps])
            return ps

        def rmsnorm(u, chunks, nk, raw, normed, gcol):
            sqs = []
            for k in range(nk):
                pk = proj(chunks[k], u)
                sq = sqb.next()
                sqs.append(sq)
                P.op("dve", lambda e, pk=pk, k=k: e.tensor_copy(out=raw[:, k, :], in_=pk[:]), r=[pk], w=[raw])
                P.op("act", lambda e, k=k, sq=sq: e.activation(out=sq[:], in_=raw[:, k, :], func=AF.Square), r=[raw], w=[sq])
            psn = psr.next()

            def mmn(e):
                for k in range(nk):
                    ins = e.matmul(psn[:], lhsT=G.ones_f[:], rhs=sqs[k][:], start=(k == 0), stop=(k == nk - 1))
                return ins
            P.op("pe", mmn, r=sqs + [G.ones_f], w=[psn])
            rs = rstd.next()
            P.op("act", lambda e: e.activation(out=rs[:], in_=psn[:], func=AF.Ln, bias=RMS_EPS, scale=1.0 / (128 * nk)), r=[psn], w=[rs])
            P.op("act", lambda e: e.activation(out=rs[:], in_=rs[:], func=AF.Exp, scale=-0.5), r=[rs], w=[rs])
            for k in range(nk):
                P.op("dve", lambda e, k=k: e.scalar_tensor_tensor(out=normed[:, k, :], in0=raw[:, k, :], scalar=gcol(k), in1=rs[:],
                                                                  op0=ALU.mult, op1=ALU.mult), r=[raw, rs], w=[normed])

        def rope(psA, psB, cst, out_ap, out_buf):
            t1, t2 = rt.next(), rt.next()
            P.op("dve", lambda e: e.tensor_tensor(out=t1[:], in0=psA[0:64, :], in1=cst[:, 0, :], op=ALU.mult), r=[psA, cst], w=[t1])
            P.op("dve", lambda e: e.tensor_tensor(out=t2[:], in0=psB[0:64, :], in1=cst[:, 1, :], op=ALU.mult), r=[psB, cst], w=[t2])
            P.op("dve", lambda e: e.tensor_tensor(out=out_ap(), in0=t1[:], in1=t2[:], op=ALU.add), r=[t1, t2], w=[out_buf])

        def load_cs(ti):
            c = cs[ti % 2]
            P.dma("sp", lambda e: e.dma_start(out=c[:, 0, :], in_=G.cosS[:, ti * NT:(ti + 1) * NT]), w=[c])
            P.dma("sp", lambda e: e.dma_start(out=c[:, 1, :], in_=G.sinS[:, ti * NT:(ti + 1) * NT]), w=[c])

        load_cs(0)
        def do_tile(ti):
            t0 = ti * NT
            if ti + 1 < ntile:
                _load_u(P, G, l, hin, hb, ub, ti + 1, 4, 3)
                load_cs(ti + 1)
            u = _make_u(P, G, l, hb, ub, ti, 4, 3)
            cst = cs[ti % 2]
            def q_heads():
              for h in range(NH):
                ps = psr.next()

                def mmq(e, ps=ps, h=h):
                    for k in range(3):
                        ins = e.matmul(ps[:], lhsT=wuq[:, 3 * h, k, :], rhs=cqn[:, k, :], start=(k == 0), stop=(k == 2))
                    return ins
                P.op("pe", mmq, r=[wuq, cqn], w=[ps])
                st = stage.next()
                P.op("act", lambda e, ps=ps, st=st: e.activation(out=st[:], in_=ps[:], func=AF.Copy), r=[ps], w=[st])
                P.dma("sp", lambda e, st=st, h=h, t0=t0: e.dma_start(out=G.qnT[h][:, t0:t0 + NT], in_=st[:]), r=[st])
                pab = []
                for v in (1, 2):
                    ps = psr.next()

                    def mmr(e, ps=ps, h=h, v=v):
                        for k in range(3):
                            ins = e.matmul(ps[0:64, :], lhsT=wuq[:, 3 * h + v, k, 0:64], rhs=cqn[:, k, :], start=(k == 0), stop=(k == 2))
                        return ins
                    P.op("pe", mmr, r=[wuq, cqn], w=[ps])
                    pab.append(ps)
                st = stage.next()
                rope(pab[0], pab[1], cst, lambda st=st: st[0:64, :], st)
                P.dma("sp", lambda e, st=st, h=h, t0=t0: e.dma_start(out=G.qrT[h][:, t0:t0 + NT], in_=st[0:64, :]), r=[st])
            def kv_heads():
              for h in range(NH):
                ps = psr.next()

                def mmk(e, ps=ps, h=h):
                    for k in range(2):
                        ins = e.matmul(ps[:], lhsT=wukn[:, h, k, :], rhs=ckvn[:, k, :], start=(k == 0), stop=(k == 1))
                    return ins
                P.op("pe", mmk, r=[wukn, ckvn], w=[ps])
                st = stage.next()
                if h % 2 == 0:
                    P.op("act", lambda e, ps=ps, st=st: e.activation(out=st[:], in_=ps[:], func=AF.Copy), r=[ps], w=[st])
                else:
                    P.op("dve", lambda e, ps=ps, st=st: e.tensor_copy(out=st[:], in_=ps[:]), r=[ps], w=[st])
                P.dma("sp", lambda e, st=st, h=h, t0=t0: e.dma_start(out=G.knT[h][:, t0:t0 + NT], in_=st[:]), r=[st])
              for tb in range(4):
                vs = vst.next()
                for half in range(2):
                    ps = psr.next()

                    def mmv(e, ps=ps, tb=tb, half=half):
                        for k in range(2):
                            ins = e.matmul(ps[:], lhsT=ckvn[:, k, tb * 128:(tb + 1) * 128], rhs=wukv[:, k, half * 512:(half + 1) * 512],
                                           start=(k == 0), stop=(k == 1))
                        return ins
                    P.op("pe", mmv, r=[wukv, ckvn], w=[ps])
                    if half == 0:
                        P.op("act", lambda e, ps=ps, vs=vs: e.activation(out=vs[:, 0:512], in_=ps[:], func=AF.Copy), r=[ps], w=[vs])
                    else:
                        P.op("dve", lambda e, ps=ps, vs=vs: e.tensor_copy(out=vs[:, 512:1024], in_=ps[:]), r=[ps], w=[vs])
                P.dma("sp", lambda e, vs=vs, tb=tb, ti=ti: e.dma_start(out=G.vm[:, :, ti * 4 + tb, :].rearrange("h p d -> p h d"),
                                                                     in_=vs[:].rearrange("p (h d) -> p h d", h=NH)), r=[vs])
            def kr_path():
                pA = proj(21, u, 64)
                pB = proj(22, u, 64)
                st = stage.next()
                rope(pA, pB, cst, lambda st=st: st[0:64, :], st)
                P.dma("sp", lambda e, st=st, t0=t0: e.dma_start(out=G.krT[:, t0:t0 + NT], in_=st[0:64, :]), r=[st])

            def gates(c0, dst):
                for j in range(8):
                    pg = proj(c0 + j, u)
                    st = stage.next()
                    P.op("act", lambda e, pg=pg, st=st: e.activation(out=st[:], in_=pg[:], func=AF.Sigmoid), r=[pg], w=[st])
                    P.dma("sp", lambda e, st=st, j=j, t0=t0, dst=dst: e.dma_start(out=dst[j * 128:(j + 1) * 128, t0:t0 + NT], in_=st[:]), r=[st])
            rmsnorm(u, [0, 1, 2], 3, cq, cqn, lambda k: G.qn[:, l, k:k + 1])
            rmsnorm(u, [3, 4], 2, ckv, ckvn, lambda k: G.kvn[:, l, k:k + 1])
            kr_path()
            gates(5, G.gaT)
            q_heads()
            gates(13, G.gbT)
            kv_heads()
        for ti in range(ntile):
            do_tile(ti)


def phase_hg(G, l):
    T = G.T
    ntile = T // NT
    with Phase(G, "hg_%d" % l) as P:
        wb = P.sb([128, 8, D], BF16)
        Sd = [P.sb([128, NH, 128], F32) for _ in range(2)]
        SR = [[Res() for _ in range(NH)] for _ in range(2)]
        qt = [P.sb([128, NH, NT], BF16) for _ in range(2)]
        kt = [P.sb([128, NH, NT], BF16) for _ in range(2)]
        vh = [P.sb([64, 8, D], BF16) for _ in range(2)]
        ab = [P.sb([128, NH, 3, 8], F32) for _ in range(2)]
        zg = [P.sb([128, 8, NT], BF16) for _ in range(2)]
        ga = [P.sb([128, 8, NT], BF16) for _ in range(2)]
        ktb = Rot([P.sb([128, 8, 64], BF16) for _ in range(2)])
        sm8 = Rot([P.sb([64, NT], BF16) for _ in range(4)])
        kT8 = Rot([P.sb([64, 8, 128], BF16) for _ in range(2)])
        kvb = Rot([P.sb([128, 8, 128], F32) for _ in range(4)])
        sbf = Rot([P.sb([128, 128], BF16) for _ in range(6)])
        sq = P.sb([128, NT], F32)
        rstd = P.sb([128, NT], F32)
        tn = P.sb([128, NT], F32)
        og = P.sb([128, NH, NT], BF16)
        ogR = [Res() for _ in range(NH)]
        mst = Rot([P.sb([128, NT], F32) for _ in range(2)])
        poR = Rot(G.ps[0:4])
        pS, pK = G.ps[4], G.ps[5]
        kvp = (G.ps[6], G.ps[7])
        psn = G.ps[4]
        yR = Rot([G.ps[4], G.ps[5]])
        P.dma("sp", lambda e: e.dma_start(out=wb[:], in_=G.WbhgS[l].rearrange("(k p) n -> p k n", p=128)), w=[wb])
        P.op("dve", lambda e: e.memset(Sd[0][:], 0.0), w=SR[0])

        def load(ti):
            s_ = ti % 2
            t0 = ti * NT
            P.dma("sp", lambda e: e.dma_start(out=qt[s_][:], in_=G.qtT[:, :, t0:t0 + NT].rearrange("h p t -> p h t")), w=[qt[s_]])
            P.dma("sp", lambda e: e.dma_start(out=kt[s_][:], in_=G.ktT[:, :, t0:t0 + NT].rearrange("h p t -> p h t")), w=[kt[s_]])
            P.dma("sp", lambda e: e.dma_start(out=vh[s_][:], in_=G.vhg[t0:t0 + NT, :].rearrange("(c s) d -> s c d", s=64)), w=[vh[s_]])
            P.dma("sp", lambda e: e.dma_start(out=ab[s_][:], in_=G.abS[ti]), w=[ab[s_]])
            P.dma("sp", lambda e: e.dma_start(out=zg[s_][:], in_=G.zgT[:, t0:t0 + NT].rearrange("(k p) t -> p k t", p=128)), w=[zg[s_]])
            P.dma("sp", lambda e: e.dma_start(out=ga[s_][:], in_=G.gaT[:, t0:t0 + NT].rearrange("(k p) t -> p k t", p=128)), w=[ga[s_]])

        def do_tile(ti):
            t0 = ti * NT
            s_ = ti % 2
            if ti + 1 < ntile:
                load(ti + 1)
            q_, k_, v_, a_, z_, g_ = qt[s_], kt[s_], vh[s_], ab[s_], zg[s_], ga[s_]

            def pre(h):
                kb_ = ktb.next()
                P.op("dve", lambda e: e.tensor_tensor(out=kb_[:], in0=k_[:, h, :].rearrange("p (c j) -> p c j", j=64),
                                                      in1=a_[:, h, 1, :].unsqueeze(2).to_broadcast([128, 8, 64]), op=ALU.mult),
                     r=[k_, a_], w=[kb_])

                def mms(e):
                    for c in range(8):
                        cs = slice(c * 64, (c + 1) * 64)
                        ins = e.matmul(pS[0:64, cs], lhsT=k_[:, h, cs], rhs=q_[:, h, cs], start=True, stop=True)
                    return ins
                P.op("pe", mms, r=[k_, q_], w=[pS])
                sm_ = sm8.next()
                P.op("dve", lambda e: e.tensor_tensor(out=sm_[:], in0=pS[0:64, :], in1=G.tri8[:], op=ALU.mult), r=[pS], w=[sm_])
                kT_ = kT8.next()
                for half in range(2):
                    def mmk(e, half=half):
                        for cc in range(4):
                            c = half * 4 + cc
                            ins = e.matmul(pK[0:64, cc * 128:(cc + 1) * 128], lhsT=kb_[:, c, :], rhs=G.ident[:], start=True, stop=True)
                        return ins
                    P.op("pe", mmk, r=[kb_], w=[pK])
                    P.op("act", lambda e, half=half: e.activation(
                        out=kT_[:, half * 4:(half + 1) * 4, :], in_=pK[0:64, :].rearrange("p (c d) -> p c d", d=128), func=AF.Copy),
                        r=[pK], w=[kT_])
                kv_ = kvb.next()
                for half in range(2):
                    def mmkv(e, half=half):
                        for cc in range(4):
                            c = half * 4 + cc
                            ins = e.matmul(kvp[half][:, cc * 128:(cc + 1) * 128], lhsT=kT_[:, c, :], rhs=v_[0:64, c, h * 128:(h + 1) * 128],
                                           start=True, stop=True)
                        return ins
                    P.op("pe", mmkv, r=[kT_, v_], w=[kvp[half]])
                    P.op("dve", lambda e, half=half: e.tensor_copy(
                        out=kv_[:, half * 4:(half + 1) * 4, :], in_=kvp[half][:].rearrange("p (c d) -> p c d", d=128)),
                        r=[kvp[half]], w=[kv_])
                return sm_, kv_

            def chain2(hs_, pres):
                pos = [poR.next() for _ in hs_]
                for c in range(8):
                    cs = slice(c * 64, (c + 1) * 64)
                    old, new_ = c % 2, (c + 1) % 2
                    for i, h in enumerate(hs_):
                        sm_, kv_ = pres[i]
                        po = pos[i]
                        sb_ = sbf.next()
                        P.op("act", lambda e, sb_=sb_, h=h, c=c, old=old: e.activation(out=sb_[:], in_=Sd[old][:, h, :], func=AF.Copy,
                                                                                       scale=a_[:, h, 0, c:c + 1]),
                             r=[SR[old][h], a_], w=[sb_])

                        def mmo(e, po=po, sm_=sm_, sb_=sb_, h=h, c=c, cs=cs):
                            e.matmul(po[:, cs], lhsT=v_[0:64, c, h * 128:(h + 1) * 128], rhs=sm_[:, cs], start=True, stop=False)
                            return e.matmul(po[:, cs], lhsT=sb_[:], rhs=q_[:, h, cs], start=False, stop=True)
                        P.op("pe", mmo, r=[v_, sm_, sb_, q_], w=[po])
                        P.op("dve", lambda e, kv_=kv_, h=h, c=c, old=old, new_=new_: e.scalar_tensor_tensor(
                            out=Sd[new_][:, h, :], in0=Sd[old][:, h, :], scalar=a_[:, h, 2, c:c + 1], in1=kv_[:, c, :],
                            op0=ALU.mult, op1=ALU.add), r=[SR[old][h], kv_, a_], w=[SR[new_][h]])
                return pos

            def norm(h, po):
                P.op("act", lambda e: e.activation(out=sq[:], in_=po[:], func=AF.Square), r=[po], w=[sq])
                P.op("pe", lambda e: e.matmul(psn[:], lhsT=G.ones_f[:], rhs=sq[:], start=True, stop=True), r=[sq], w=[psn])
                P.op("act", lambda e: e.activation(out=rstd[:], in_=psn[:], func=AF.Ln, bias=RMS_EPS, scale=1.0 / 128), r=[psn], w=[rstd])
                P.op("act", lambda e: e.activation(out=rstd[:], in_=rstd[:], func=AF.Exp, scale=-0.5), r=[rstd], w=[rstd])
                P.op("dve", lambda e: e.tensor_tensor(out=tn[:], in0=po[:], in1=rstd[:], op=ALU.mult), r=[po, rstd], w=[tn])
                P.op("dve", lambda e: e.scalar_tensor_tensor(out=og[:, h, :], in0=tn[:], scalar=G.hgn[:, l, h:h + 1], in1=z_[:, h, :],
                                                             op0=ALU.mult, op1=ALU.mult), r=[tn, z_], w=[ogR[h]])

            pairs = [(0, 1), (2, 3), (4, 5), (6, 7)]
            nxt = [pre(h) for h in pairs[0]]
            prev = None
            for pi, pr_ in enumerate(pairs):
                cur = nxt
                if pi + 1 < len(pairs):
                    nxt = [pre(h) for h in pairs[pi + 1]]
                pos = chain2(pr_, cur)
                if prev is not None:
                    for h, po in prev:
                        norm(h, po)
                prev = list(zip(pr_, pos))
            for h, po in prev:
                norm(h, po)
            for m in range(8):
                py = yR.next()

                def mmy(e, py=py, m=m):
                    for h in range(NH):
                        ins = e.matmul(py[:], lhsT=wb[:, h, m * 128:(m + 1) * 128], rhs=og[:, h, :], start=(h == 0), stop=(h == NH - 1))
                    return ins
                P.op("pe", mmy, r=[wb] + ogR, w=[py])
                ms = mst.next()
                P.op("dve", lambda e, py=py, ms=ms, m=m: e.tensor_tensor(out=ms[:], in0=py[:], in1=g_[:, m, :], op=ALU.mult), r=[py, g_], w=[ms])
                P.dma("sp", lambda e, ms=ms, m=m: e.dma_start(out=G.mhT[m * 128:(m + 1) * 128, t0:t0 + NT], in_=ms[:]), r=[ms])

        load(0)
        for ti in range(ntile):
            do_tile(ti)


def phase_a1(G, l):
    T = G.T
    nq = T // NT
    nkb = T // 128
    with Phase(G, "a1_%d" % l) as P:
        kr = P.sb([64, T], BF16)
        kn = [P.sb([128, T], BF16) for _ in range(2)]
        vv = [P.sb([128, nkb, 128], BF16) for _ in range(2)]
        qn = [P.sb([128, NT], BF16) for _ in range(2)]
        qr = [P.sb([64, NT], BF16) for _ in range(2)]
        pT = Rot([P.sb([128, NT], BF16) for _ in range(6)])
        rinv = P.sb([128, NT], F32)
        ost = Rot([P.sb([128, NT], BF16) for _ in range(2)])
        sR, oR, rR = Rot(G.ps[0:4]), Rot(G.ps[4:6]), Rot(G.ps[6:8])
        P.dma("sp", lambda e: e.dma_start(out=kr[:], in_=G.krT), w=[kr])
        if l + 1 < G.depth:
            cast_layer(P, G, l + 1)
        cnt = [0]

        def load_head(h):
            P.dma("sp", lambda e: e.dma_start(out=kn[h % 2][:], in_=G.knT[h]), w=[kn[h % 2]])
            P.dma("sp", lambda e: e.dma_start(out=vv[h % 2][:], in_=G.vm[h]), w=[vv[h % 2]])

        def load_q(h, qi, slot):
            P.dma("sp", lambda e: e.dma_start(out=qn[slot][:], in_=G.qnT[h][:, qi * NT:(qi + 1) * NT]), w=[qn[slot]])
            P.dma("sp", lambda e: e.dma_start(out=qr[slot][:], in_=G.qrT[h][:, qi * NT:(qi + 1) * NT]), w=[qr[slot]])

        LOOK = 2

        def do_q(h, qi, slot):
            kn_, vv_, qn_, qr_ = kn[h % 2], vv[h % 2], qn[slot], qr[slot]
            po, pr = oR.next(), rR.next()
            nb = 4 * qi + 4
            pend = []

            def emit_o(kb, p_, off):
                def mmo(e):
                    e.matmul(po[:, off:NT], lhsT=vv_[:, kb, :], rhs=p_[:, off:NT], start=(kb == 0), stop=(kb == nb - 1))
                    return e.matmul(pr[:, off:NT], lhsT=G.ones_b[:], rhs=p_[:, off:NT], start=(kb == 0), stop=(kb == nb - 1))
                P.op("pe", mmo, r=[vv_, p_], w=[po, pr])

            for kb in range(nb):
                d = kb - 4 * qi
                off = 128 * d if d > 0 else 0
                ks = slice(kb * 128, (kb + 1) * 128)
                pS = sR.next()
                p_ = pT.next()

                def mms(e, pS=pS, off=off, ks=ks):
                    e.matmul(pS[:, off:NT], lhsT=kn_[:, ks], rhs=qn_[:, off:NT], start=True, stop=False)
                    return e.matmul(pS[:, off:NT], lhsT=kr[0:64, ks], rhs=qr_[0:64, off:NT], start=False, stop=True)
                P.op("pe", mms, r=[kn_, kr, qn_, qr_], w=[pS])
                P.op("act", lambda e, pS=pS, p_=p_, off=off: e.activation(out=p_[:, off:NT], in_=pS[:, off:NT], func=AF.Exp, scale=ATT_SCALE),
                     r=[pS], w=[p_])
                if d >= 0:
                    P.op("dve", lambda e, p_=p_, off=off: e.tensor_tensor(out=p_[:, off:off + 128], in0=p_[:, off:off + 128], in1=G.tri[:],
                                                                          op=ALU.mult), r=[p_], w=[p_])
                pend.append((kb, p_, off))
                if len(pend) > LOOK:
                    emit_o(*pend.pop(0))
            while pend:
                emit_o(*pend.pop(0))
            P.op("act", lambda e: e.activation(out=rinv[:], in_=pr[:], func=AF.Ln), r=[pr], w=[rinv])
            P.op("act", lambda e: e.activation(out=rinv[:], in_=rinv[:], func=AF.Exp, scale=-1.0), r=[rinv], w=[rinv])
            os_ = ost.next()
            P.op("dve", lambda e: e.tensor_tensor(out=os_[:], in0=po[:], in1=rinv[:], op=ALU.mult), r=[po, rinv], w=[os_])
            P.dma("sp", lambda e: e.dma_start(out=G.omT[h][:, qi * NT:(qi + 1) * NT], in_=os_[:]), r=[os_])

        load_head(0)
        load_q(0, 0, 0)
        seq = [(h, qi) for h in range(NH) for qi in range(nq)]
        for i, (h, qi) in enumerate(seq):
            if qi == 0 and h + 1 < NH:
                load_head(h + 1)
            if i + 1 < len(seq):
                load_q(seq[i + 1][0], seq[i + 1][1], (i + 1) % 2)
            do_q(h, qi, i % 2)


def phase_a2(G, l, hin, hout):
    T = G.T
    ntile = T // NT
    mc = G.modc
    with Phase(G, "a2_%d" % l) as P:
        wbm = P.sb([128, 8, D], BF16)
        wo = P.sb([128, 8, D], BF16)
        om = [P.sb([128, NH, NT], BF16) for _ in range(2)]
        mh = [P.sb([128, 8, NT], F32) for _ in range(2)]
        gb = [P.sb([128, 8, NT], BF16) for _ in range(2)]
        hb = [P.sb([128, 8, NT], F32) for _ in range(2)]
        hR = [[Res() for _ in range(8)] for _ in range(2)]
        mg = P.sb([128, 8, NT], BF16)
        mgR = [Res() for _ in range(8)]
        mt = Rot([P.sb([128, NT], F32) for _ in range(2)])
        sqb = [P.sb([128, NT], F32) for _ in range(2)]
        st = {k: P.sb([128, NT], F32) for k in ("mean", "msq", "rstd")}
        yR = Rot(G.ps[0:6])
        ps_s, ps_q = G.ps[6], G.ps[7]
        P.dma("sp", lambda e: e.dma_start(out=wbm[:], in_=G.WbmlaS[l].rearrange("(k p) n -> p k n", p=128)), w=[wbm])
        P.dma("sp", lambda e: e.dma_start(out=wo[:], in_=G.WoutS[l].rearrange("(k p) n -> p k n", p=128)), w=[wo])

        def load(ti):
            s_ = ti % 2
            t0 = ti * NT
            P.dma("sp", lambda e: e.dma_start(out=om[s_][:], in_=G.omT[:, :, t0:t0 + NT].rearrange("h p t -> p h t")), w=[om[s_]])
            P.dma("sp", lambda e: e.dma_start(out=mh[s_][:], in_=G.mhT[:, t0:t0 + NT].rearrange("(k p) t -> p k t", p=128)), w=[mh[s_]])
            P.dma("sp", lambda e: e.dma_start(out=gb[s_][:], in_=G.gbT[:, t0:t0 + NT].rearrange("(k p) t -> p k t", p=128)), w=[gb[s_]])
            P.dma("sp", lambda e: e.dma_start(out=hb[s_][:], in_=hin[:, t0:t0 + NT].rearrange("(k p) t -> p k t", p=128)), w=hR[s_])

        def do_tile(ti):
            s_ = ti % 2
            t0 = ti * NT
            if ti + 1 < ntile:
                load(ti + 1)
            om_, mh_, gb_, h_ = om[s_], mh[s_], gb[s_], hb[s_]
            for m in range(8):
                py = yR.next()

                def mm1(e, py=py, m=m):
                    for h in range(NH):
                        ins = e.matmul(py[:], lhsT=wbm[:, h, m * 128:(m + 1) * 128], rhs=om_[:, h, :], start=(h == 0), stop=(h == NH - 1))
                    return ins
                P.op("pe", mm1, r=[wbm, om_], w=[py])
                t_ = mt.next()
                P.op("dve", lambda e, py=py, t_=t_, m=m: e.tensor_tensor(out=t_[:], in0=py[:], in1=gb_[:, m, :], op=ALU.mult), r=[py, gb_], w=[t_])
                P.op("dve", lambda e, t_=t_, m=m: e.tensor_tensor(out=mg[:, m, :], in0=t_[:], in1=mh_[:, m, :], op=ALU.add), r=[t_, mh_], w=[mgR[m]])
                P.op("dve", lambda e, m=m: e.tensor_scalar(out=h_[:, m, :], in0=h_[:, m, :], scalar1=ALPHA, scalar2=None, op0=ALU.mult),
                     r=[hR[s_][m]], w=[hR[s_][m]])
            for m in range(8):
                py = yR.next()

                def mm2(e, py=py, m=m):
                    for k in range(8):
                        ins = e.matmul(py[:], lhsT=wo[:, k, m * 128:(m + 1) * 128], rhs=mg[:, k, :], start=(k == 0), stop=(k == 7))
                    return ins
                P.op("pe", mm2, r=[wo] + mgR, w=[py])
                P.op("dve", lambda e, py=py, m=m: e.scalar_tensor_tensor(out=h_[:, m, :], in0=py[:], scalar=mc[:, l, 5, m:m + 1], in1=h_[:, m, :],
                                                                         op0=ALU.mult, op1=ALU.add), r=[py, hR[s_][m]], w=[hR[s_][m]])
            rch = [((lambda m=m: h_[:, m, :]), hR[s_][m]) for m in range(8)]
            layer_norm_tile(P, G, l, 1, rch, ps_s, ps_q, sqb, st)
            P.dma("sp", lambda e: e.dma_start(out=hout[:, t0:t0 + NT].rearrange("(k p) t -> p k t", p=128), in_=h_[:]), r=hR[s_])

        load(0)
        for ti in range(ntile):
            do_tile(ti)


_PROG_CACHE = {}


def kernel(**inputs):
    B, S, _ = inputs["x"].shape
    depth = inputs["ada_w"].shape[0]
    key = (S, depth)
    if key not in _PROG_CACHE:
        _PROG_CACHE[key] = build_program(S, depth)
    nc, G = _PROG_CACHE[key]
    shared = prep_shared(inputs, depth)
    n_cores = 8
    in_maps = []
    for c in range(n_cores):
        m = dict(shared)
        m.update(prep_core(inputs, c % B))
        in_maps.append(m)
    res = run_bass_kernel_spmd(nc, in_maps, core_ids=list(range(n_cores)))
    out = np.stack([np.ascontiguousarray(res.results[b]["outT"].T) for b in range(B)], axis=0)
    return out.astype(np.float32)
```
